# Optimizing a Trainium2 kernel written in Bass

```python
import jax, jax.numpy as jnp
from jax import lax
import numpy as np

D_MODEL = 1024
BATCH = 8
SEQ = 4096
DEPTH = 1

PLE_DIM = 256
N_ATTN_HEADS = 8
HEAD_DIM = 64
ATTN_WIDTH = N_ATTN_HEADS * HEAD_DIM
SSD_HEADS = 8
SSD_HEAD_DIM = 64
SSD_WIDTH = SSD_HEADS * SSD_HEAD_DIM
SSD_STATE = 128
CONV_WIDTH = 4
CONV_CH = SSD_WIDTH + 2 * SSD_STATE
CHUNK = 128
MIX_WIDTH = ATTN_WIDTH + SSD_WIDTH
IN_PROJ_WIDTH = 3 * ATTN_WIDTH + SSD_WIDTH + CONV_CH + SSD_HEADS
D_FF = 4 * D_MODEL
ROPE_THETA = 10000.0
DILATED_BRANCHES = ((128, 1), (512, 4), (2048, 16))
ATTN_BLOCK = 128
NORM_EPS = 1e-6

kernel_name = 'hybrid_ssd_dilated_attention_layer'


def rms_norm(x, g):
    xf = x.astype(jnp.float32)
    xf = xf * lax.rsqrt(jnp.mean(xf * xf, axis=-1, keepdims=True) + NORM_EPS)
    return xf.astype(x.dtype) * g


def apply_rope(t, positions):
    dh = t.shape[-1]
    half = dh // 2
    inv_freq = ROPE_THETA ** (-jnp.arange(half, dtype=jnp.float32) * 2.0 / dh)
    ang = positions.astype(jnp.float32)[:, :, None] * inv_freq
    cos = jnp.cos(ang)[:, :, None, :]
    sin = jnp.sin(ang)[:, :, None, :]
    tf = t.astype(jnp.float32)
    t1, t2 = tf[..., :half], tf[..., half:]
    return jnp.concatenate([t1 * cos - t2 * sin, t2 * cos + t1 * sin], axis=-1).astype(t.dtype)


def dilated_branch(q, k, v, window, dilation):
    b, s, nh, dh = q.shape
    span = window // dilation
    blk = ATTN_BLOCK
    sub_len = s // dilation
    nb = -(-sub_len // blk)
    sub_pad = nb * blk

    def to_sub(t):
        t = t.reshape(b, sub_len, dilation, nh, dh).transpose(0, 2, 1, 3, 4)
        return jnp.pad(t, ((0, 0), (0, 0), (0, sub_pad - sub_len), (0, 0), (0, 0)))

    def band(t):
        t = jnp.pad(t, ((0, 0), (0, 0), (blk, 0), (0, 0), (0, 0)))
        t = t.reshape(b, dilation, nb + 1, blk, nh, dh)
        return jnp.concatenate([t[:, :, :-1], t[:, :, 1:]], axis=3)

    qb = to_sub(q).reshape(b, dilation, nb, blk, nh, dh)
    kb = band(to_sub(k))
    vb = band(to_sub(v))

    scores = jnp.einsum('brnqhd,brnkhd->brnhqk', qb, kb).astype(jnp.float32)
    qi = jnp.arange(blk)[:, None]
    ki = jnp.arange(2 * blk)[None, :]
    dist = qi + blk - ki
    key_idx = jnp.arange(nb)[:, None, None] * blk - blk + ki
    valid = (dist >= 0) & (dist <= span) & (key_idx >= 0)
    scores = jnp.where(valid[None, None, :, None], scores, -jnp.inf)
    m = jnp.max(scores, axis=-1, keepdims=True)
    e = jnp.exp(scores - m)
    den = jnp.sum(e, axis=-1, keepdims=True)
    out = jnp.einsum('brnhqk,brnkhd->brnqhd', (e / den).astype(v.dtype), vb)
    lse = (m + jnp.log(den))[..., 0]

    out = out.reshape(b, dilation, sub_pad, nh, dh)[:, :, :sub_len]
    out = out.transpose(0, 2, 1, 3, 4).reshape(b, s, nh, dh)
    lse = lse.transpose(0, 1, 2, 4, 3).reshape(b, dilation, sub_pad, nh)[:, :, :sub_len]
    lse = lse.transpose(0, 2, 1, 3).reshape(b, s, nh)
    return out, lse


def causal_conv(u, w, bias):
    y = lax.conv_general_dilated(u, w[:, None, :], window_strides=(1,),
                                 padding=[(CONV_WIDTH - 1, 0)],
                                 dimension_numbers=('NWC', 'WIO', 'NWC'),
                                 feature_group_count=u.shape[-1])
    return y + bias


def segsum_exp(a):
    t = a.shape[-1]
    cs = jnp.cumsum(a, axis=-1)
    diff = cs[..., :, None] - cs[..., None, :]
    mask = jnp.tril(jnp.ones((t, t), dtype=bool))
    return jnp.exp(jnp.where(mask, diff, -jnp.inf))


def ssd_chunked(xdt, adt, bm, cm):
    b, s, nh, hp = xdt.shape
    n = bm.shape[-1]
    c = s // CHUNK
    x_c = xdt.reshape(b, c, CHUNK, nh, hp)
    a_c = adt.reshape(b, c, CHUNK, nh).transpose(0, 3, 1, 2)
    b_c = bm.reshape(b, c, CHUNK, n)
    c_c = cm.reshape(b, c, CHUNK, n)
    a_cs = jnp.cumsum(a_c, axis=-1)

    decay = segsum_exp(a_c)
    cb = jnp.einsum('bcln,bcsn->bcls', c_c, b_c)
    y_diag = jnp.einsum('bcls,bhcls,bcshp->bclhp', cb, decay, x_c)

    decay_states = jnp.exp(a_cs[..., -1:] - a_cs)
    states = jnp.einsum('bcln,bhcl,bclhp->bchpn', b_c, decay_states, x_c)
    chunk_decay = jnp.exp(a_cs[..., -1])

    def step(carry, inp):
        st, dec = inp
        return carry * dec[..., None, None] + st, carry

    init = jnp.zeros_like(states[:, 0])
    _, prev = lax.scan(step, init, (states.transpose(1, 0, 2, 3, 4), chunk_decay.transpose(2, 0, 1)))
    prev = prev.transpose(1, 0, 2, 3, 4)
    y_off = jnp.einsum('bcln,bchpn,bhcl->bclhp', c_c, prev, jnp.exp(a_cs))
    return (y_diag + y_off).reshape(b, s, nh, hp)


def hybrid_mixer(u, positions, w_in, conv_w, conv_b, dt_bias, a_log, d_skip, ssd_norm_g, w_out):
    b, s, _ = u.shape
    proj = u @ w_in
    splits = [ATTN_WIDTH, 2 * ATTN_WIDTH, 3 * ATTN_WIDTH,
              3 * ATTN_WIDTH + SSD_WIDTH, 3 * ATTN_WIDTH + SSD_WIDTH + CONV_CH]
    q, k, v, z, xbc, dt = jnp.split(proj, splits, axis=-1)

    q = apply_rope(q.reshape(b, s, N_ATTN_HEADS, HEAD_DIM), positions) * (HEAD_DIM ** -0.5)
    k = apply_rope(k.reshape(b, s, N_ATTN_HEADS, HEAD_DIM), positions)
    v = v.reshape(b, s, N_ATTN_HEADS, HEAD_DIM)
    outs, lses = [], []
    for window, dilation in DILATED_BRANCHES:
        o, l = dilated_branch(q, k, v, window, dilation)
        outs.append(o)
        lses.append(l)
    alpha = jax.nn.softmax(jnp.stack(lses, axis=0), axis=0)
    attn = jnp.einsum('absh,abshd->bshd', alpha, jnp.stack(outs, axis=0).astype(jnp.float32))
    attn = attn.reshape(b, s, ATTN_WIDTH).astype(u.dtype)

    xbc = jax.nn.silu(causal_conv(xbc, conv_w, conv_b))
    xs, bm, cm = jnp.split(xbc, [SSD_WIDTH, SSD_WIDTH + SSD_STATE], axis=-1)
    xs = xs.reshape(b, s, SSD_HEADS, SSD_HEAD_DIM).astype(jnp.float32)
    dt = jax.nn.softplus(dt.astype(jnp.float32) + dt_bias)
    a = -jnp.exp(a_log.astype(jnp.float32))
    y = ssd_chunked(xs * dt[..., None], dt * a, bm.astype(jnp.float32), cm.astype(jnp.float32))
    y = y + d_skip[:, None] * xs
    y = rms_norm(y.reshape(b, s, SSD_WIDTH) * jax.nn.silu(z.astype(jnp.float32)), ssd_norm_g)
    y = y.astype(u.dtype)

    return jnp.concatenate([attn, y], axis=-1) @ w_out


def setup_inputs(seed: int = 0) -> dict:
    key = jax.random.key(seed)
    ks = jax.random.split(key, 24)
    f32 = jnp.float32

    def gain(k, n):
        return 1.0 + 0.01 * jax.random.normal(k, (DEPTH, n), f32)

    x = jax.random.normal(ks[0], (BATCH, SEQ, D_MODEL), f32)
    p = jax.random.normal(ks[1], (DEPTH, BATCH, SEQ, PLE_DIM), f32)
    offset = jax.random.randint(ks[2], (BATCH, 1), 0, 1024, dtype=jnp.int32)
    positions = (jnp.arange(SEQ, dtype=jnp.int32)[None, :] + offset).astype(jnp.int32)

    w_in = jax.random.normal(ks[3], (DEPTH, D_MODEL, IN_PROJ_WIDTH), f32) * D_MODEL ** -0.5
    conv_w = jax.random.normal(ks[4], (DEPTH, CONV_WIDTH, CONV_CH), f32) * CONV_WIDTH ** -0.5
    conv_b = 0.01 * jax.random.normal(ks[5], (DEPTH, CONV_CH), f32)
    dt0 = jnp.exp(jax.random.uniform(ks[6], (DEPTH, SSD_HEADS), f32, np.log(1e-3), np.log(1e-1)))
    dt_bias = dt0 + jnp.log(-jnp.expm1(-dt0))
    a_log = jnp.log(jax.random.uniform(ks[7], (DEPTH, SSD_HEADS), f32, 1.0, 16.0))
    d_skip = 1.0 + 0.01 * jax.random.normal(ks[8], (DEPTH, SSD_HEADS), f32)
    w_out = jax.random.normal(ks[9], (DEPTH, MIX_WIDTH, D_MODEL), f32) * MIX_WIDTH ** -0.5
    w_up = jax.random.normal(ks[10], (DEPTH, D_MODEL, D_FF), f32) * D_MODEL ** -0.5
    w_down = jax.random.normal(ks[11], (DEPTH, D_FF, D_MODEL), f32) * D_FF ** -0.5
    w_ple_gate = jax.random.normal(ks[12], (DEPTH, D_MODEL, D_MODEL), f32) * D_MODEL ** -0.5
    w_ple_proj = jax.random.normal(ks[13], (DEPTH, PLE_DIM, D_MODEL), f32) * PLE_DIM ** -0.5

    return {
        'x': x, 'p': p, 'positions': positions,
        'norm_mix_pre': gain(ks[14], D_MODEL), 'norm_mix_post': gain(ks[15], D_MODEL),
        'w_in': w_in, 'conv_w': conv_w, 'conv_b': conv_b, 'dt_bias': dt_bias,
        'a_log': a_log, 'd_skip': d_skip, 'ssd_norm_g': gain(ks[16], SSD_WIDTH),
        'w_out': w_out,
        'norm_mlp_pre': gain(ks[17], D_MODEL), 'norm_mlp_post': gain(ks[18], D_MODEL),
        'w_up': w_up, 'w_down': w_down,
        'w_ple_gate': w_ple_gate, 'w_ple_proj': w_ple_proj,
        'norm_ple_post': gain(ks[19], D_MODEL),
    }


def reference(x, p, positions, norm_mix_pre, norm_mix_post, w_in, conv_w, conv_b, dt_bias,
              a_log, d_skip, ssd_norm_g, w_out, norm_mlp_pre, norm_mlp_post, w_up, w_down,
              w_ple_gate, w_ple_proj, norm_ple_post):
    h = x
    for i in range(DEPTH):
        u = rms_norm(h, norm_mix_pre[i])
        mix = hybrid_mixer(u, positions, w_in[i], conv_w[i], conv_b[i], dt_bias[i], a_log[i],
                           d_skip[i], ssd_norm_g[i], w_out[i])
        h = h + rms_norm(mix, norm_mix_post[i])
        u = rms_norm(h, norm_mlp_pre[i])
        ff = jnp.square(jax.nn.relu(u @ w_up[i])) @ w_down[i]
        h = h + rms_norm(ff, norm_mlp_post[i])
        ple = (p[i] @ w_ple_proj[i]) * jax.nn.sigmoid(h @ w_ple_gate[i])
        h = h + rms_norm(ple, norm_ple_post[i])
    return h
```

```python
import os
import numpy as np
from contextlib import ExitStack
import concourse.bass as bass
import concourse.mybir as mybir
from concourse.bass_utils import run_bass_kernel_spmd

F32 = mybir.dt.float32
BF16 = mybir.dt.bfloat16
I32 = mybir.dt.int32
AF = mybir.ActivationFunctionType
ALU = mybir.AluOpType

S = 4096
D = 1024
NT = S // 128
NIN = 2824
NEG = -30000.0
EPS = 1e-6
TWO_PI = 2.0 * np.pi * (1.0 - 2.5e-7)
SAME_ENG_SYNC = os.environ.get("KSES", "1") == "1"


class Sem:
    def __init__(self, h):
        self.h = h
        self.val = 0


class Eng:
    def __init__(self, name, e, sem):
        self.name = name
        self.e = e
        self.sem = sem
        self.seen = {}


class Buf:
    __slots__ = ("w", "r", "name")

    def __init__(self, name=""):
        self.w = None
        self.r = {}
        self.name = name


class _Proxy:
    def __init__(self):
        self.call = None

    def __getattr__(self, name):
        def f(*a, **k):
            self.call = (name, a, k)
            return self
        return f


class Ctx:
    def __init__(self, nc, es):
        self.nc = nc
        self.rec = None
        self.vt = 0.0
        self.log = {}

        def mk(n):
            return Sem(es.enter_context(nc.semaphore(n)))

        self.pe = Eng("pe", nc.tensor, mk("s_pe"))
        self.act = Eng("act", nc.scalar, mk("s_act"))
        self.dve = Eng("dve", nc.vector, mk("s_dve"))
        self.pool = Eng("pool", nc.gpsimd, mk("s_pool"))
        self.sp = Eng("sp", nc.sync, mk("s_sp"))
        self.engs = [self.pe, self.act, self.dve, self.pool, self.sp]
        self.dma_sems = {"sp": [mk(f"d_sp{i}") for i in range(12)],
                         "pool": [mk(f"d_pl{i}") for i in range(6)]}
        self.dma_rr = {"sp": 0, "pool": 0}

    def _wait(self, E, sem, val):
        if val <= 0 or E.seen.get(sem, 0) >= val:
            return
        E.e.wait_ge(sem.h, val)
        E.seen[sem] = val
        self.log.setdefault(E.name, []).append(("w", sem, val))

    def _deps(self, E, reads, writes):
        deps = {}

        def add(ev):
            if ev is None:
                return
            s, v = ev
            if deps.get(s, 0) < v:
                deps[s] = v

        for b in reads:
            add(b.w)
            if b.name.startswith("ps"):
                for s_, ev in b.r.items():
                    if s_ is not E.sem:
                        add(ev)
        for b in writes:
            add(b.w)
            for ev in b.r.values():
                add(ev)
        for s, v in deps.items():
            if s is E.sem and (E is self.pe or not SAME_ENG_SYNC):
                continue
            self._wait(E, s, v)

    def _mark(self, ev, reads, writes):
        s = ev[0]
        for b in reads:
            old = b.r.get(s)
            if old is None or old[1] < ev[1]:
                b.r[s] = ev
        for b in writes:
            b.w = ev
            b.r = {}

    def op(self, E, fn, reads=(), writes=(), signal=True):
        pr = _Proxy()
        fn(pr)
        if self.rec is not None:
            self.rec.append((self.vt, len(self.rec), "op", E, pr.call, list(reads), list(writes), signal))
        else:
            self._emit_op(E, pr.call, reads, writes, signal)

    def _emit_op(self, E, call, reads, writes, signal):
        self._deps(E, reads, writes)
        name, a, k = call
        ins = getattr(E.e, name)(*a, **k)
        if signal:
            E.sem.val += 1
            ins.then_inc(E.sem.h, 1)
            ev = (E.sem, E.sem.val)
            self.log.setdefault(E.name, []).append(("i", E.sem, 1, name))
        else:
            assert E is self.pe
            ev = (E.sem, E.sem.val + 1)
        self._mark(ev, reads, writes)

    def dma(self, q, out, in_, reads=(), writes=(), **kw):
        if self.rec is not None:
            self.rec.append((self.vt, len(self.rec), "dma", q, out, in_, list(reads), list(writes), kw))
        else:
            self._emit_dma(q, out, in_, reads, writes, kw)

    def _emit_dma(self, q, out, in_, reads, writes, kw):
        E = self.sp if q == "sp" else self.pool
        self._deps(E, reads, writes)
        sems = self.dma_sems[q]
        i = self.dma_rr[q]
        self.dma_rr[q] = (i + 1) % len(sems)
        s = sems[i]
        self._wait(E, s, s.val)
        s.val += 16
        E.e.dma_start(out=out, in_=in_, **kw).then_inc(s.h, 16)
        self.log.setdefault(E.name, []).append(("i", s, 16, "dma"))
        self._mark((s, s.val), reads, writes)

    def check_deadlock(self):
        vals = {}
        pos = {k: 0 for k in self.log}
        progress = True
        while progress:
            progress = False
            for k, lst in self.log.items():
                while pos[k] < len(lst):
                    it = lst[pos[k]]
                    if it[0] == "w":
                        if vals.get(it[1], 0) >= it[2]:
                            pos[k] += 1
                            progress = True
                        else:
                            break
                    else:
                        vals[it[1]] = vals.get(it[1], 0) + it[2]
                        pos[k] += 1
                        progress = True
        bad = {k: (pos[k], len(lst)) for k, lst in self.log.items() if pos[k] < len(lst)}
        if bad:
            msg = []
            names = {}
            for e in self.engs:
                names[e.sem] = "sem_" + e.name
            for q, ss in self.dma_sems.items():
                for i, s_ in enumerate(ss):
                    names[s_] = f"dma_{q}{i}"
            for k, (p_, n_) in bad.items():
                it = self.log[k][p_]
                msg.append(f"{k} blocked at {p_}/{n_}: wait {names.get(it[1])} >= {it[2]} (have {vals.get(it[1], 0)})")
            raise RuntimeError("DEADLOCK: " + "; ".join(msg))

    def begin_rec(self):
        self.rec = []
        self.vt = 0.0

    def at(self, vt):
        self.vt = vt

    def flush(self):
        rec = self.rec
        self.rec = None
        rec.sort(key=lambda r: (r[0], r[1]))
        for r in rec:
            if r[2] == "op":
                self._emit_op(*r[3:])
            else:
                self._emit_dma(*r[3:])

    def barrier(self):
        allsems = [e.sem for e in self.engs] + self.dma_sems["sp"] + self.dma_sems["pool"]
        for E in self.engs:
            for s in allsems:
                if s is E.sem:
                    continue
                self._wait(E, s, s.val)


def build_nc(debug=False, phases="ABC"):
    KSKIP = os.environ.get("KSKIP", "").split(",")
    nc = bass.Bass("TRN2", target_bir_lowering=False)

    def din(name, shape, dt=F32):
        return nc.dram_tensor(name, list(shape), dt, kind="ExternalInput").ap()

    x_d = din("x", [S, D])
    p_d = din("p", [S, 256])
    pos_d = din("pos", [128, NT], I32)
    g_mix_pre_d = din("norm_mix_pre", [1, D])
    g_mix_post_d = din("norm_mix_post", [1, D])
    w_in_d = din("w_in", [D, NIN])
    conv_w_d = din("conv_w", [128, 6, 4])
    conv_b_d = din("conv_b", [128, 6])
    dt_bias_d = din("dt_bias", [1, 8])
    a_log_d = din("a_log", [1, 8])
    d_skip_d = din("d_skip", [1, 8])
    g_ssd_d = din("ssd_norm_g", [1, 512])
    w_out_d = din("w_out", [D, D])
    g_mlp_pre_pk_d = din("norm_mlp_pre_pk", [128, 8])
    g_mlp_post_d = din("norm_mlp_post", [1, D])
    w_up_d = din("w_up", [D, 4096])
    w_down_d = din("w_down", [4096, D])
    w_gate_d = din("w_ple_gate", [D, D])
    w_proj_d = din("w_ple_proj", [256, D])
    g_ple_post_d = din("norm_ple_post", [1, D])
    out_d = nc.dram_tensor("out", [S, D], F32, kind="ExternalOutput").ap()

    qkvT_s = nc.dram_tensor("qkvT_s", [3, 4, 128, S], BF16, kind="Internal").ap()
    catT_s = nc.dram_tensor("catT_s", [NT, 128, 8, 128], BF16, kind="Internal").ap()
    w_up_s = nc.dram_tensor("w_up_s", [D, 4096], BF16, kind="Internal").ap()
    w_down_s = nc.dram_tensor("w_down_s", [4096, D], BF16, kind="Internal").ap()
    dbg = {}
    if debug:
        dbg["qkvT"] = nc.dram_tensor("dbg_qkvT", [3, 4, 128, S], BF16, kind="ExternalOutput").ap()
        dbg["catT"] = nc.dram_tensor("dbg_catT", [NT, 128, 8, 128], BF16, kind="ExternalOutput").ap()

    with ExitStack() as es:
        c = Ctx(nc, es)
        pe, act, dve, pool = c.pe, c.act, c.dve, c.pool

        def sb(name, shape, dt, stack=es):
            return stack.enter_context(nc.sbuf_tensor(name, list(shape), dt))

        def psum(name, shape, dt, stack=es):
            return stack.enter_context(nc.psum_tensor(name, list(shape), dt))

        ident_bf = sb("ident_bf", [128, 128], BF16)
        w_out_bf = sb("w_out_bf", [128, 8, D], BF16)
        w_gate_bf = sb("w_gate_bf", [128, 8, D], BF16)
        w_proj_bf = sb("w_proj_bf", [128, 2, D], BF16)
        g_mix_post = sb("g_mix_post", [128, D], F32)
        g_mlp_post = sb("g_mlp_post", [128, D], F32)
        g_ple_post = sb("g_ple_post", [128, D], F32)
        esAB = ExitStack()
        es.enter_context(esAB)
        cF = sb("cF", [128, 128], F32, esAB)
        tri_f = sb("tri_f", [128, 128], F32, esAB)
        sl_f = sb("sl_f", [128, 128], F32, esAB)
        ones_f = sb("ones_f", [128, 128], F32, esAB)
        ssdmask_bf = sb("ssdmask_bf", [128, 128], BF16, esAB)
        maskAT_bf = sb("maskAT_bf", [128, 256], BF16, esAB)
        maskAT_f = sb("maskAT_f", [128, 256], F32, esAB)
        sel2_f = sb("sel2_f", [2, 128], F32, esAB)
        B_const = Buf("const")

        def pconst(fn, rd=True):
            c.op(pool, fn, reads=[B_const] if rd else [], writes=[B_const])

        pconst(lambda e: e.memset(cF[:], 1.0))
        pconst(lambda e: e.affine_select(out=cF[:], in_=cF[:], pattern=[[-1, 128]], compare_op=ALU.is_equal,
                                         fill=0.0, base=0, channel_multiplier=1))
        pconst(lambda e: e.tensor_copy(out=ident_bf[:], in_=cF[:]))
        pconst(lambda e: e.memset(tri_f[:], 1.0))
        pconst(lambda e: e.affine_select(out=tri_f[:], in_=tri_f[:], pattern=[[1, 128]], compare_op=ALU.is_ge,
                                         fill=0.0, base=0, channel_multiplier=-1))
        pconst(lambda e: e.memset(sl_f[:], 1.0))
        pconst(lambda e: e.affine_select(out=sl_f[:], in_=sl_f[:], pattern=[[-1, 128]], compare_op=ALU.is_gt,
                                         fill=0.0, base=0, channel_multiplier=1))
        pconst(lambda e: e.memset(ones_f[:], 1.0))
        pconst(lambda e: e.memset(cF[:], 0.0))
        pconst(lambda e: e.affine_select(out=cF[:], in_=cF[:], pattern=[[1, 128]], compare_op=ALU.is_ge,
                                         fill=NEG, base=0, channel_multiplier=-1))
        pconst(lambda e: e.tensor_copy(out=ssdmask_bf[:], in_=cF[:]))
        pconst(lambda e: e.memset(maskAT_f[:], 1.0))
        pconst(lambda e: e.affine_select(out=maskAT_f[:], in_=maskAT_f[:], pattern=[[1, 256]], compare_op=ALU.is_ge,
                                         fill=0.0, base=0, channel_multiplier=-1))
        pconst(lambda e: e.affine_select(out=maskAT_f[:], in_=maskAT_f[:], pattern=[[-1, 256]], compare_op=ALU.is_ge,
                                         fill=0.0, base=128, channel_multiplier=1))
        pconst(lambda e: e.tensor_copy(out=maskAT_bf[:], in_=maskAT_f[:]))
        pconst(lambda e: e.memset(sel2_f[:], 1.0))
        pconst(lambda e: e.affine_select(out=sel2_f[:, 0:64], in_=sel2_f[:, 0:64], pattern=[[0, 64]], compare_op=ALU.is_ge,
                                         fill=0.0, base=0, channel_multiplier=-1))
        pconst(lambda e: e.affine_select(out=sel2_f[:, 64:128], in_=sel2_f[:, 64:128], pattern=[[0, 64]],
                                         compare_op=ALU.is_ge, fill=0.0, base=-1, channel_multiplier=1))

        def bcast_load(dst, src_row, n, buf):
            c.dma("sp", dst[:, 0:n], src_row[0:1, 0:n].partition_broadcast(128), writes=[buf])


        B_wg = Buf("wg")
        B_gains = Buf("gains")
        B_out = Buf("out")
        def load_small_weights():
            for wt, wd, nk in ((w_out_bf, w_out_d, 8), (w_gate_bf, w_gate_d, 8), (w_proj_bf, w_proj_d, 2)):
                wv_ = wd.rearrange("(k p) n -> p k n", p=128)
                for k0 in range(0, nk, 4):
                    k1 = min(nk, k0 + 4)
                    c.dma("pool", wt[:, k0:k1, :], wv_[:, k0:k1, :], writes=[B_wg])

        if "A" not in phases:
            load_small_weights()
        for gt, gd in ((g_mix_post, g_mix_post_d), (g_mlp_post, g_mlp_post_d), (g_ple_post, g_ple_post_d)):
            bcast_load(gt, gd, D, B_gains)

        B_wus = [Buf(f"wus{k}") for k in range(8)]
        B_wds = [Buf(f"wds{k}") for k in range(8)]
        B_qkv_tiles = [Buf(f"qkvs{t}") for t in range(NT)]
        B_catY_tiles = [Buf(f"catY{t}") for t in range(NT)]
        B_catA_tiles = [Buf(f"catA{t}") for t in range(NT)]
        if "A" in phases:
          with ExitStack() as ph:
            w_in_bf = sb("w_in_bf", [128, 8, NIN], BF16, ph)
            B_win = Buf("w_in")
            w_in_v = w_in_d.rearrange("(k p) n -> p k n", p=128)
            for k in range(8):
                c.dma("pool", w_in_bf[:, k, :], w_in_v[:, k, :], writes=[B_win])
            load_small_weights()
            gpre = sb("gpre", [128, D], F32, ph)
            gssd = sb("gssd", [128, 512], F32, ph)
            convw = sb("convw", [128, 6, 4], F32, ph)
            convb = sb("convb", [128, 6], F32, ph)
            sm = sb("sm", [128, 64], F32, ph)
            B_par = Buf("params")
            bcast_load(gpre, g_mix_pre_d, D, B_par)
            bcast_load(gssd, g_ssd_d, 512, B_par)
            c.dma("sp", convw[:], conv_w_d[:, :, :], writes=[B_par])
            c.dma("sp", convb[:], conv_b_d[:, :], writes=[B_par])
            c.dma("sp", sm[:, 0:8], dt_bias_d[0:1, :].partition_broadcast(128), writes=[B_par])
            c.dma("sp", sm[:, 8:16], a_log_d[0:1, :].partition_broadcast(128), writes=[B_par])
            c.dma("sp", sm[:, 16:24], d_skip_d[0:1, :].partition_broadcast(128), writes=[B_par])
            c.op(act, lambda e: e.activation(out=sm[:, 24:32], in_=sm[:, 8:16], func=AF.Exp),
                 reads=[B_par], writes=[B_par])
            c.op(dve, lambda e: e.tensor_scalar(out=sm[:, 24:32], in0=sm[:, 24:32], scalar1=-1.0, scalar2=None,
                                               op0=ALU.mult), reads=[B_par], writes=[B_par])
            dtb_bc, dsk_bc, A_bc = sm[:, 0:8], sm[:, 16:24], sm[:, 24:32]

            pos_i = sb("pos_i", [128, NT], I32, ph)
            posf = sb("posf", [128, NT], F32, ph)
            invf = sb("invf", [128, 32], F32, ph)
            X = sb("ropeX", [128, NT, 32], F32, ph)
            Xc = sb("ropeXc", [128, NT, 32], F32, ph)
            Ki = sb("ropeKi", [128, NT, 32], I32, ph)
            Kf = sb("ropeKf", [128, NT, 32], F32, ph)
            cos2 = sb("cos2", [128, NT, 64], F32, ph)
            sin2 = sb("sin2", [128, NT, 64], F32, ph)
            B_rope = Buf("rope")
            c.dma("sp", pos_i[:], pos_d[:, :], writes=[B_rope])
            invf_lo = sb("invf_lo", [128, 32], F32, ph)
            for j in range(32):
                val = 10000.0 ** (-(2.0 * j) / 64.0) / (2.0 * np.pi)
                hi_ = float(np.float32(val))
                lo_ = float(np.float32(val - float(np.float32(val))))
                c.op(pool, lambda e, j=j, v_=hi_: e.memset(invf[:, j:j + 1], v_), writes=[B_rope])
                c.op(pool, lambda e, j=j, v_=lo_: e.memset(invf_lo[:, j:j + 1], v_), writes=[B_rope])
            c.op(dve, lambda e: e.tensor_copy(out=posf[:], in_=pos_i[:]), reads=[B_rope], writes=[B_rope])
            c.op(dve, lambda e: e.tensor_tensor(out=X[:], in0=posf[:, :, None].to_broadcast([128, NT, 32]),
                                                in1=invf[:, None, :].to_broadcast([128, NT, 32]), op=ALU.mult),
                 reads=[B_rope], writes=[B_rope])
            c.op(dve, lambda e: e.tensor_tensor(out=Xc[:], in0=posf[:, :, None].to_broadcast([128, NT, 32]),
                                                in1=invf_lo[:, None, :].to_broadcast([128, NT, 32]), op=ALU.mult),
                 reads=[B_rope], writes=[B_rope])
            c.op(dve, lambda e: e.tensor_tensor(out=X[:], in0=X[:], in1=Xc[:], op=ALU.add),
                 reads=[B_rope], writes=[B_rope])
            c.op(dve, lambda e: e.tensor_copy(out=Ki[:], in_=X[:]), reads=[B_rope], writes=[B_rope])
            c.op(dve, lambda e: e.tensor_copy(out=Kf[:], in_=Ki[:]), reads=[B_rope], writes=[B_rope])
            c.op(dve, lambda e: e.tensor_tensor(out=Kf[:], in0=X[:], in1=Kf[:], op=ALU.subtract),
                 reads=[B_rope], writes=[B_rope])
            c.op(act, lambda e: e.activation(out=sin2[:, :, 32:64], in_=Kf[:], func=AF.Sin, scale=TWO_PI),
                 reads=[B_rope], writes=[B_rope])
            c.op(act, lambda e: e.activation(out=sin2[:, :, 0:32], in_=Kf[:], func=AF.Sin, scale=-TWO_PI),
                 reads=[B_rope], writes=[B_rope])
            c.op(dve, lambda e: e.tensor_scalar(out=Xc[:], in0=X[:], scalar1=0.25, scalar2=None, op0=ALU.add),
                 reads=[B_rope], writes=[B_rope])
            c.op(dve, lambda e: e.tensor_copy(out=Ki[:], in_=Xc[:]), reads=[B_rope], writes=[B_rope])
            c.op(dve, lambda e: e.tensor_copy(out=Kf[:], in_=Ki[:]), reads=[B_rope], writes=[B_rope])
            c.op(dve, lambda e: e.tensor_tensor(out=Kf[:], in0=Xc[:], in1=Kf[:], op=ALU.subtract),
                 reads=[B_rope], writes=[B_rope])
            c.op(act, lambda e: e.activation(out=cos2[:, :, 0:32], in_=Kf[:], func=AF.Sin, scale=TWO_PI),
                 reads=[B_rope], writes=[B_rope])
            c.op(act, lambda e: e.activation(out=cos2[:, :, 32:64], in_=Kf[:], func=AF.Sin, scale=TWO_PI),
                 reads=[B_rope], writes=[B_rope])

            xt = [sb(f"xt{i}", [128, D], F32, ph) for i in range(2)]
            B_xt = [Buf("xt0"), Buf("xt1")]
            junk = sb("junk", [128, D], BF16, ph)
            junkB = sb("junkB", [128, 512], BF16, ph)
            B_junk, B_junkB = Buf("junk"), Buf("junkB")
            stF = sb("statF", [128, 8], F32, ph)
            stB = sb("statB", [128, 8], F32, ph)
            B_stF, B_stB = Buf("statF"), Buf("statB")
            u_bf = sb("u_bf", [128, D], BF16, ph)
            B_u = Buf("u")
            uT = sb("uT", [128, 8, 128], BF16, ph)
            B_uT = Buf("uT")
            tA = sb("tA", [128, 512], F32, ph)
            tB = sb("tB", [128, 512], F32, ph)
            B_tA, B_tB = Buf("tA"), Buf("tB")
            qkv_bf = sb("qkv_bf", [128, 3, 512], BF16, ph)
            B_qkv = [Buf("q_bf"), Buf("k_bf"), Buf("v_bf")]
            T3 = sb("T3", [128, 3, 4, 128], BF16, ph)
            B_T3 = [Buf("T3q"), Buf("T3k"), Buf("T3v")]
            zs2 = [sb(f"zs{i}", [128, 512], F32, ph) for i in range(2)]
            B_zs2 = [Buf("zs0"), Buf("zs1")]
            pre = sb("pre", [128, 6, 132], BF16, ph)
            B_pre = Buf("pre")
            Wd = sb("Wd", [128, 6, 4, 128], BF16, ph)
            B_Wd = Buf("Wd")
            for cc in range(6):
                for j in range(4):
                    c.op(dve, lambda e: e.tensor_scalar(out=Wd[:, cc, j, :], in0=ident_bf[:], scalar1=convw[:, cc, j:j + 1],
                                                        scalar2=None, op0=ALU.mult),
                         reads=[B_const, B_par], writes=[B_Wd])
            cacc = sb("cacc", [128, 6, 128], F32, ph)
            B_cacc = Buf("cacc")
            ctmp = sb("ctmp", [128, 6, 128], F32, ph)
            B_ctmp = Buf("ctmp")
            xbcT2 = [sb(f"xbcT_bf{i}", [128, 6, 128], BF16, ph) for i in range(2)]
            B_xbcT2 = [Buf("xbcT0"), Buf("xbcT1")]
            dtraw2 = [sb(f"dtraw{i}", [128, 8], F32, ph) for i in range(2)]
            B_dtraw2 = [Buf("dtraw0"), Buf("dtraw1")]
            xsB = sb("xsB", [128, 5, 128], BF16, ph)
            B_xsB = Buf("xsB")
            dtt = sb("dtt", [128, 64], F32, ph)
            B_dtt = Buf("dtt")
            L_all = sb("L_all", [128, 8, 128], F32, ph)
            B_L = Buf("L")
            decT = sb("decT", [128, 8, 128], F32, ph)
            B_dec = Buf("dec")
            GT = sb("GT", [128, 8, 128], BF16, ph)
            B_GT = Buf("GT")
            xdt = sb("xdt", [128, 512], BF16, ph)
            xdtd = sb("xdtd", [128, 512], BF16, ph)
            B_xdt, B_xdtd = Buf("xdt"), Buf("xdtd")
            state = sb("state", [128, 512], F32, ph)
            state_bf = sb("state_bf", [128, 512], BF16, ph)
            B_state, B_statebf = Buf("state"), Buf("statebf")
            y1 = sb("y1", [128, 512], F32, ph)
            y2 = sb("y2", [128, 512], F32, ph)
            B_y1, B_y2 = Buf("y1"), Buf("y2")
            yn_bf = sb("yn_bf", [128, 512], BF16, ph)
            B_yn = Buf("yn")
            yT_st = sb("yT_st", [128, 4, 128], BF16, ph)
            B_yT = Buf("yT")

            psT = psum("psT", [128, 1024], BF16, ph)
            psP = [psum(f"psP{i}", [128, 512], F32, ph) for i in range(4)]
            psW = [psum(f"psW{i}", [128, 512], F32, ph) for i in range(3)]
            B_psT = Buf("psT")
            B_psP = [Buf(f"psP{i}") for i in range(4)]
            B_psW = [Buf(f"psW{i}") for i in range(3)]

            c.op(pool, lambda e: e.memset(pre[:, :, 0:3], 0.0), writes=[B_pre])
            c.op(dve, lambda e: e.memset(state[:], 0.0), writes=[B_state])
            c.op(dve, lambda e: e.memset(state_bf[:], 0.0), writes=[B_statebf])

            PA = float(os.environ.get("KPA", "2.5"))
            if "noreca" not in KSKIP:
                c.begin_rec()

            def front(t):
                T0 = t * PA
                xb, Bx = xt[t % 2], B_xt[t % 2]
                zs, B_zs = zs2[t % 2], B_zs2[t % 2]
                xbcT_bf, B_xbcT = xbcT2[t % 2], B_xbcT2[t % 2]
                dtraw, B_dtraw = dtraw2[t % 2], B_dtraw2[t % 2]
                c.at(T0 - 3.9)
                c.dma("sp", xb[:], x_d[t * 128:(t + 1) * 128, :], writes=[Bx])
                c.at(T0 - 2.4)
                c.op(dve, lambda e: e.memset(stF[:, 0:1], 0.0), writes=[B_stF])
                c.op(act, lambda e: e.activation(out=junk[:], in_=xb[:], func=AF.Square, accum_out=stF[:, 0:1]),
                     reads=[Bx], writes=[B_stF, B_junk])
                c.op(act, lambda e: e.activation(out=stF[:, 1:2], in_=stF[:, 0:1], func=AF.Ln, scale=1.0 / D, bias=EPS),
                     reads=[B_stF], writes=[B_stF])
                c.op(act, lambda e: e.activation(out=stF[:, 2:3], in_=stF[:, 1:2], func=AF.Exp, scale=-0.5),
                     reads=[B_stF], writes=[B_stF])
                c.op(dve, lambda e: e.scalar_tensor_tensor(out=u_bf[:], in0=xb[:], scalar=stF[:, 2:3], in1=gpre[:],
                                                           op0=ALU.mult, op1=ALU.mult),
                     reads=[Bx, B_stF, B_par], writes=[B_u])
                c.at(T0 + 0.0)
                for k in range(8):
                    c.op(pe, lambda e: e.transpose(psT[:, k * 128:(k + 1) * 128], u_bf[:, k * 128:(k + 1) * 128], ident_bf[:]),
                         reads=[B_u, B_const], writes=[B_psT], signal=(k == 7))
                c.op(act, lambda e: e.copy(out=uT[:], in_=psT[:].rearrange("p (k i) -> p k i", k=8)),
                     reads=[B_psT], writes=[B_uT])

                def inproj(pt, Bp, col0, vt):
                    c.at(vt)
                    for k in range(8):
                        c.op(pe, lambda e: e.matmul(pt[:], lhsT=uT[:, k, :], rhs=w_in_bf[:, k, col0:col0 + 512],
                                                    start=(k == 0), stop=(k == 7)),
                             reads=[B_uT, B_win], writes=[Bp], signal=(k == 7))

                inproj(psP[0], B_psP[0], 0, T0 + 0.3)
                inproj(psP[1], B_psP[1], 512, T0 + 0.3)
                c.at(T0 + 0.5)
                for qi in range(2):
                    pt, Bp = psP[qi], B_psP[qi]
                    pv = pt[:].rearrange("p (h d) -> p h d", h=8)
                    c.op(dve, lambda e: e.tensor_tensor(out=tA[:].rearrange("p (h d) -> p h d", h=8), in0=pv,
                                                        in1=cos2[:, t, None, :].to_broadcast([128, 8, 64]), op=ALU.mult),
                         reads=[Bp, B_rope], writes=[B_tA])
                    tBv = tB[:].rearrange("p (h d) -> p h d", h=8)
                    c.op(dve, lambda e: e.tensor_tensor(out=tBv[:, :, 0:32], in0=pv[:, :, 32:64],
                                                        in1=sin2[:, t, None, 0:32].to_broadcast([128, 8, 32]), op=ALU.mult),
                         reads=[Bp, B_rope], writes=[B_tB])
                    c.op(dve, lambda e: e.tensor_tensor(out=tBv[:, :, 32:64], in0=pv[:, :, 0:32],
                                                        in1=sin2[:, t, None, 32:64].to_broadcast([128, 8, 32]), op=ALU.mult),
                         reads=[Bp, B_rope], writes=[B_tB])
                    c.op(dve, lambda e: e.tensor_tensor(out=qkv_bf[:, qi, :], in0=tA[:], in1=tB[:], op=ALU.add),
                         reads=[B_tA, B_tB], writes=[B_qkv[qi]])
                inproj(psP[2], B_psP[2], 1024, T0 + 0.6)
                inproj(psP[3], B_psP[3], 1536, T0 + 0.6)
                c.at(T0 + 0.8)
                c.op(act, lambda e: e.copy(out=qkv_bf[:, 2, :], in_=psP[2][:]), reads=[B_psP[2]], writes=[B_qkv[2]])
                c.op(act, lambda e: e.activation(out=zs[:], in_=psP[3][:], func=AF.Silu), reads=[B_psP[3]], writes=[B_zs])
                c.at(T0 + 1.0)
                for k in range(8):
                    c.op(pe, lambda e: e.matmul(psP[1][:, 256:264], lhsT=uT[:, k, :], rhs=w_in_bf[:, k, 2816:2824],
                                                start=(k == 0), stop=(k == 7)),
                         reads=[B_uT, B_win], writes=[B_psP[1]], signal=False)
                for cc in range(6):
                    for k in range(8):
                        dst = psP[0][:, cc * 128:(cc + 1) * 128] if cc < 4 else psP[1][:, (cc - 4) * 128:(cc - 3) * 128]
                        c.op(pe, lambda e: e.matmul(dst, lhsT=w_in_bf[:, k, 2048 + cc * 128:2048 + (cc + 1) * 128],
                                                    rhs=uT[:, k, :], start=(k == 0), stop=(k == 7)),
                             reads=[B_uT, B_win], writes=[B_psP[0] if cc < 4 else B_psP[1]],
                             signal=((cc == 3 or cc == 5) and k == 7))
                c.at(T0 + 1.3)
                c.op(act, lambda e: e.copy(out=pre[:, 0:4, 3:131], in_=psP[0][:].rearrange("p (k i) -> p k i", k=4)),
                     reads=[B_psP[0]], writes=[B_pre])
                c.op(act, lambda e: e.copy(out=pre[:, 4:6, 3:131], in_=psP[1][:, 0:256].rearrange("p (k i) -> p k i", k=2)),
                     reads=[B_psP[1]], writes=[B_pre])
                c.op(dve, lambda e: e.tensor_tensor(out=dtraw[:], in0=psP[1][:, 256:264], in1=dtb_bc, op=ALU.add),
                     reads=[B_psP[1], B_par], writes=[B_dtraw])
                for cc in range(6):
                    dstc = psP[2][:, cc * 128:(cc + 1) * 128] if cc < 4 else psP[3][:, (cc - 4) * 128:(cc - 3) * 128]
                    for j in range(4):
                        c.op(pe, lambda e: e.matmul(dstc, lhsT=Wd[:, cc, j, :], rhs=pre[:, cc, j:j + 128],
                                                    start=(j == 0), stop=(j == 3)),
                             reads=[B_pre, B_Wd], writes=[B_psP[2] if cc < 4 else B_psP[3]],
                             signal=((cc == 3 or cc == 5) and j == 3))
                c.op(pool, lambda e: e.tensor_copy(out=pre[:, :, 0:3], in_=pre[:, :, 128:131]),
                     reads=[B_pre], writes=[B_pre])
                c.at(T0 + 1.6)
                for qi in range(3):
                    for hp in range(4):
                        c.op(pe, lambda e: e.transpose(psT[:, hp * 128:(hp + 1) * 128], qkv_bf[:, qi, hp * 128:(hp + 1) * 128],
                                                       ident_bf[:]),
                             reads=[B_qkv[qi], B_const], writes=[B_psT], signal=(hp == 3))
                    c.op(act, lambda e: e.copy(out=T3[:, qi, :, :], in_=psT[:, 0:512].rearrange("p (k i) -> p k i", k=4)),
                         reads=[B_psT], writes=[B_T3[qi]])
                    c.dma("sp", qkvT_s[qi, :, :, t * 128:(t + 1) * 128].rearrange("h p i -> p h i"), T3[:, qi, :, :],
                          reads=[B_T3[qi]], writes=[B_qkv_tiles[t]])
                c.at(T0 + 2.0)
                for cc in range(6):
                    srcc = psP[2][:, cc * 128:(cc + 1) * 128] if cc < 4 else psP[3][:, (cc - 4) * 128:(cc - 3) * 128]
                    c.op(act, lambda e: e.activation(out=xbcT_bf[:, cc, :], in_=srcc, func=AF.Silu, bias=convb[:, cc:cc + 1]),
                         reads=[B_psP[2] if cc < 4 else B_psP[3], B_par], writes=[B_xbcT])

            def back(t):
                T0 = t * PA + 3.0
                zs, B_zs = zs2[t % 2], B_zs2[t % 2]
                xbcT_bf, B_xbcT = xbcT2[t % 2], B_xbcT2[t % 2]
                dtraw, B_dtraw = dtraw2[t % 2], B_dtraw2[t % 2]
                BT = xbcT_bf[:, 4, :]
                CT = xbcT_bf[:, 5, :]
                xs_tok = xsB[:, 0:4, :].rearrange("p k (a d) -> p (k a) d", d=64)
                B_tok = xsB[:, 4, :]
                c.at(T0 + 0.0)
                c.op(act, lambda e: e.activation(out=dtt[:, 48:56], in_=dtraw[:], func=AF.Exp), reads=[B_dtraw], writes=[B_dtt])
                c.op(act, lambda e: e.activation(out=dtt[:, 0:8], in_=dtt[:, 48:56], func=AF.Ln, bias=1.0),
                     reads=[B_dtt], writes=[B_dtt])
                c.op(dve, lambda e: e.tensor_tensor(out=dtt[:, 8:16], in0=dtt[:, 0:8], in1=A_bc, op=ALU.mult),
                     reads=[B_dtt, B_par], writes=[B_dtt])
                for i, m in enumerate((tri_f, sl_f, ones_f)):
                    c.op(pe, lambda e: e.matmul(psW[2][:, 8 * i:8 + 8 * i], lhsT=m[:], rhs=dtt[:, 8:16], start=True, stop=True),
                         reads=[B_dtt, B_const], writes=[B_psW[2]], signal=(i == 2))
                c.op(act, lambda e: e.activation(out=dtt[:, 16:40], in_=psW[2][:, 0:24], func=AF.Exp),
                     reads=[B_psW[2]], writes=[B_dtt])
                ecs, dstate, cdec = dtt[:, 16:24], dtt[:, 24:32], dtt[:, 32:40]
                c.op(dve, lambda e: e.tensor_tensor(out=dtt[:, 40:48], in0=dtt[:, 0:8], in1=dstate, op=ALU.mult),
                     reads=[B_dtt], writes=[B_dtt])
                c.op(dve, lambda e: e.tensor_tensor(out=L_all[:], in0=sl_f[:, None, :].to_broadcast([128, 8, 128]),
                                                     in1=dtt[:, 8:16, None].to_broadcast([128, 8, 128]), op=ALU.mult),
                     reads=[B_dtt, B_const], writes=[B_L])
                c.at(T0 + 0.2)
                for cc in range(5):
                    c.op(pe, lambda e: e.transpose(psT[:, cc * 128:(cc + 1) * 128], xbcT_bf[:, cc, :], ident_bf[:]),
                         reads=[B_xbcT, B_const], writes=[B_psT], signal=(cc == 4))
                c.op(act, lambda e: e.copy(out=xsB[:], in_=psT[:, 0:640].rearrange("p (k i) -> p k i", k=5)),
                     reads=[B_psT], writes=[B_xsB])
                c.at(T0 + 0.4)
                for h in range(8):
                    pd, Bpd = psW[h // 4], B_psW[h // 4]
                    dst = pd[:, (h % 4) * 128:(h % 4 + 1) * 128]
                    c.op(pe, lambda e: e.matmul(dst, lhsT=L_all[:, h, :], rhs=tri_f[:], start=True, stop=False),
                         reads=[B_L, B_const], writes=[Bpd], signal=False)
                    c.op(pe, lambda e: e.matmul(dst, lhsT=ident_bf[:], rhs=ssdmask_bf[:], start=False, stop=True),
                         reads=[B_const], writes=[Bpd], signal=(h % 4 == 3))
                c.op(act, lambda e: e.activation(out=decT[:, 0:4, :], in_=psW[0][:].rearrange("p (h l) -> p h l", h=4),
                                                 func=AF.Exp), reads=[B_psW[0]], writes=[B_dec])
                c.op(act, lambda e: e.activation(out=decT[:, 4:8, :], in_=psW[1][:].rearrange("p (h l) -> p h l", h=4),
                                                 func=AF.Exp), reads=[B_psW[1]], writes=[B_dec])
                c.op(pe, lambda e: e.matmul(psW[2][:, 128:256], lhsT=BT, rhs=CT, start=True, stop=True),
                     reads=[B_xbcT], writes=[B_psW[2]])
                c.op(dve, lambda e: e.tensor_tensor(out=GT[:], in0=psW[2][:, None, 128:256].to_broadcast([128, 8, 128]),
                                                    in1=decT[:], op=ALU.mult),
                     reads=[B_psW[2], B_dec], writes=[B_GT])
                c.op(dve, lambda e: e.tensor_tensor(out=xdt[:].rearrange("p (h d) -> p h d", h=8), in0=xs_tok,
                                                     in1=dtt[:, 0:8, None].to_broadcast([128, 8, 64]), op=ALU.mult),
                     reads=[B_xsB, B_dtt], writes=[B_xdt])
                c.op(dve, lambda e: e.tensor_tensor(out=xdtd[:].rearrange("p (h d) -> p h d", h=8), in0=xs_tok,
                                                     in1=dtt[:, 40:48, None].to_broadcast([128, 8, 64]), op=ALU.mult),
                     reads=[B_xsB, B_dtt], writes=[B_xdtd])
                c.op(dve, lambda e: e.tensor_tensor(out=y2[:].rearrange("p (h d) -> p h d", h=8), in0=xs_tok,
                                                     in1=dsk_bc[:, :, None].to_broadcast([128, 8, 64]), op=ALU.mult),
                     reads=[B_xsB, B_par], writes=[B_y2])
                c.at(T0 + 0.9)
                for h in range(8):
                    c.op(pe, lambda e: e.matmul(psW[0][:, h * 64:(h + 1) * 64], lhsT=GT[:, h, :],
                                                rhs=xdt[:, h * 64:(h + 1) * 64], start=True, stop=True),
                         reads=[B_GT, B_xdt], writes=[B_psW[0]], signal=(h == 7))
                c.op(pe, lambda e: e.matmul(psW[1][:], lhsT=CT, rhs=state_bf[:], start=True, stop=True),
                     reads=[B_xbcT, B_statebf], writes=[B_psW[1]])
                c.op(pe, lambda e: e.matmul(psW[2][:], lhsT=B_tok, rhs=xdtd[:], start=True, stop=True),
                     reads=[B_xsB, B_xdtd], writes=[B_psW[2]])
                c.at(T0 + 1.1)
                c.op(dve, lambda e: e.tensor_tensor(out=y1[:].rearrange("p (h d) -> p h d", h=8),
                                                    in0=psW[1][:].rearrange("p (h d) -> p h d", h=8),
                                                    in1=ecs[:, :, None].to_broadcast([128, 8, 64]), op=ALU.mult),
                     reads=[B_psW[1], B_dtt], writes=[B_y1])
                c.op(dve, lambda e: e.tensor_tensor(out=y1[:], in0=psW[0][:], in1=y1[:], op=ALU.add),
                     reads=[B_psW[0], B_y1], writes=[B_y1])
                c.op(dve, lambda e: e.tensor_tensor(out=state[:].rearrange("p (h d) -> p h d", h=8),
                                                    in0=state[:].rearrange("p (h d) -> p h d", h=8),
                                                    in1=cdec[:, :, None].to_broadcast([128, 8, 64]), op=ALU.mult),
                     reads=[B_state, B_dtt], writes=[B_state])
                c.op(dve, lambda e: e.tensor_tensor(out=state[:], in0=psW[2][:], in1=state[:], op=ALU.add),
                     reads=[B_psW[2], B_state], writes=[B_state])
                c.op(dve, lambda e: e.tensor_copy(out=state_bf[:], in_=state[:]), reads=[B_state], writes=[B_statebf])
                c.op(dve, lambda e: e.tensor_tensor(out=y1[:], in0=y1[:], in1=y2[:], op=ALU.add),
                     reads=[B_y1, B_y2], writes=[B_y1])
                c.op(dve, lambda e: e.tensor_tensor(out=y1[:], in0=y1[:], in1=zs[:], op=ALU.mult),
                     reads=[B_y1, B_zs], writes=[B_y1])
                c.op(dve, lambda e: e.memset(stB[:, 4:5], 0.0), writes=[B_stB])
                c.op(act, lambda e: e.activation(out=junkB[:], in_=y1[:], func=AF.Square, accum_out=stB[:, 4:5]),
                     reads=[B_y1], writes=[B_stB, B_junkB])
                c.op(act, lambda e: e.activation(out=stB[:, 5:6], in_=stB[:, 4:5], func=AF.Ln, scale=1.0 / 512, bias=EPS),
                     reads=[B_stB], writes=[B_stB])
                c.op(act, lambda e: e.activation(out=stB[:, 6:7], in_=stB[:, 5:6], func=AF.Exp, scale=-0.5),
                     reads=[B_stB], writes=[B_stB])
                c.op(dve, lambda e: e.scalar_tensor_tensor(out=yn_bf[:], in0=y1[:], scalar=stB[:, 6:7], in1=gssd[:],
                                                           op0=ALU.mult, op1=ALU.mult),
                     reads=[B_y1, B_stB, B_par], writes=[B_yn])
                c.at(T0 + 2.5)
                for cc in range(4):
                    c.op(pe, lambda e: e.transpose(psT[:, cc * 128:(cc + 1) * 128], yn_bf[:, cc * 128:(cc + 1) * 128], ident_bf[:]),
                         reads=[B_yn, B_const], writes=[B_psT], signal=(cc == 3))
                c.op(act, lambda e: e.copy(out=yT_st[:], in_=psT[:, 0:512].rearrange("p (k i) -> p k i", k=4)),
                     reads=[B_psT], writes=[B_yT])
                c.dma("sp", catT_s[t, :, 4:8, :], yT_st[:], reads=[B_yT], writes=[B_catY_tiles[t]])

            for t in range(NT):
                front(t)
                back(t)
                if t < 8:
                    c.at(t * PA + 2.2)
                    c.dma("pool", w_up_s[t * 128:(t + 1) * 128, :], w_up_d[t * 128:(t + 1) * 128, :], writes=[B_wus[t]])
                elif t < 16:
                    k4 = t - 8
                    c.at(t * PA + 2.2)
                    c.dma("pool", w_down_s[k4 * 512:(k4 + 1) * 512, :], w_down_d[k4 * 512:(k4 + 1) * 512, :],
                          writes=[B_wds[k4]])
            if "noreca" not in KSKIP:
                c.flush()
            c.barrier()

        if "B" in phases:
          with ExitStack() as ph:
            qTh = [sb(f"qz{i}", [128, 2, S], BF16, ph) for i in range(2)]
            kTh = [sb(f"kTh{i}", [128, S], BF16, ph) for i in range(2)]
            vTh = [sb(f"vTh{i}", [128, S], BF16, ph) for i in range(2)]
            B_qh = [[Buf(f"qkv{i}_{pp}") for pp in range(2)] for i in range(3)]
            acc_all = sb("acc_all", [128, 2, S], F32, ph)
            v_aug2 = [sb(f"v_aug{i}", [128, 32, 196], BF16, ph) for i in range(2)]
            B_vd2 = [Buf("vd0"), Buf("vd1")]
            attnT = sb("attnT", [128, S], BF16, ph)
            rd = sb("rd", [128, 512], F32, ph)
            NU = 3
            PTr = [sb(f"PTr{i}", [128, 2, 256], BF16, ph) for i in range(NU)]
            PT = [sb(f"PT{i}", [128, 2, 256], BF16, ph) for i in range(NU)]
            B_acc, B_attnT, B_rd = Buf("acc"), Buf("attnT"), Buf("rd")
            B_PTr = [Buf(f"PTr{i}") for i in range(NU)]
            B_PT = [Buf(f"PT{i}") for i in range(NU)]
            psTa = psum("psTa", [128, 1024], BF16, ph)
            psSs = [psum(f"psS{i}", [128, 2, 256], F32, ph) for i in range(NU)]
            psO = [psum(f"psO{i}", [128, 512], F32, ph) for i in range(2)]
            psR = psum("psR", [128, 512], F32, ph)
            psR2 = psum("psR2", [128, 512], F32, ph)
            B_psR2 = Buf("psR2")
            B_psTa = Buf("psTa")
            B_psTx = [B_psTa, B_psTa]
            B_psSs = [Buf(f"psS{i}") for i in range(NU)]
            B_psO = [Buf("psO0"), Buf("psO1")]
            B_psR = Buf("psR")
            psTx = [psTa, psTa]
            for i_ in range(2):
                c.op(dve, lambda e: e.memset(psO[i_][:], 0.0), writes=[B_psO[i_]])
            for v_aug, B_vd in zip(v_aug2, B_vd2):
                c.op(pool, lambda e: e.memset(v_aug[:], 0.0), writes=[B_vd])
                c.op(pool, lambda e: e.memset(v_aug[:, :, 64:65], 1.0), reads=[B_vd], writes=[B_vd])
                c.op(pool, lambda e: e.memset(v_aug[:, :, 100:101], 1.0), reads=[B_vd], writes=[B_vd])
            for pp in range(2):
                c.op(pool, lambda e: e.memset(qTh[pp][64:128, 0, :], 0.0), writes=[B_qh[0][pp]])
                c.op(pool, lambda e: e.memset(qTh[pp][0:64, 1, :], 0.0), writes=[B_qh[0][pp]])
            if "norec" not in KSKIP:
                c.begin_rec()

            def load_hp(hp, vt):
                c.at(vt)
                pp = hp % 2
                c.dma("sp", qTh[pp][0:64, 0, :], qkvT_s[0, hp, 0:64, :], reads=B_qkv_tiles, writes=[B_qh[0][pp]])
                c.dma("sp", qTh[pp][64:128, 1, :], qkvT_s[0, hp, 64:128, :], reads=B_qkv_tiles, writes=[B_qh[0][pp]])
                c.dma("sp", kTh[pp][:], qkvT_s[1, hp, :, :], reads=B_qkv_tiles, writes=[B_qh[1][pp]])
                c.dma("sp", vTh[pp][:], qkvT_s[2, hp, :, :], reads=B_qkv_tiles, writes=[B_qh[2][pp]])

            load_hp(0, -10.0)
            u = 0
            for hp in range(4):
                if hp + 1 < 4:
                    load_hp(hp + 1, u + 4.0)
                qT_, kT_, vT_ = qTh[hp % 2], kTh[hp % 2], vTh[hp % 2]
                Bq_, Bk_, Bv_ = B_qh[0][hp % 2], B_qh[1][hp % 2], B_qh[2][hp % 2]
                for di, d in enumerate((1, 4, 16)):
                    nb = 32 // d
                    g_ = hp * 3 + di
                    v_aug, B_vd = v_aug2[g_ % 2], B_vd2[g_ % 2]

                    def cols(r, a, n, d=d):
                        st_ = r + d * a
                        return slice(st_, st_ + d * (n - 1) + 1, d)

                    c.at(32.0 * (g_ - 1) + 2.2 if g_ > 0 else -1.0)
                    for blk in range(0 if "novaug" in KSKIP else 32):
                        r, j = blk // nb, blk % nb
                        grp = (blk // 4) % 2
                        c.op(pe, lambda e: e.transpose(psTx[grp][:, (blk % 4) * 128:(blk % 4 + 1) * 128],
                                                       vT_[:, cols(r, 128 * j, 128)], ident_bf[:]),
                             reads=[Bv_, B_const], writes=[B_psTx[grp]], signal=(blk % 4 == 3))
                        if blk % 4 == 3:
                            src = psTx[grp][:, 0:512].rearrange("p (k i) -> p k i", k=4)
                            c.op(act, lambda e: e.copy(out=v_aug[:, blk - 3:blk + 1, 0:64], in_=src[:, :, 0:64]),
                                 reads=[B_psTx[grp]], writes=[B_vd])
                            c.op(act, lambda e: e.copy(out=v_aug[:, blk - 3:blk + 1, 132:196], in_=src[:, :, 64:128]),
                                 reads=[B_psTx[grp]], writes=[B_vd])
                    for r in range(d):
                        for j in range(nb):
                            blk = r * nb + j
                            nq = 256 if j + 1 < nb else 128
                            Sp, BSp = psSs[u % NU], B_psSs[u % NU]
                            Pr, BPr = PTr[u % NU], B_PTr[u % NU]
                            Pt, BPt = PT[u % NU], B_PT[u % NU]
                            kc = cols(r, 128 * j, 128)
                            qc = cols(r, 128 * j, nq)
                            c.at(float(u))
                            if nq == 256:
                                c.op(pe, lambda e: e.matmul(Sp[:, :, 0:nq], lhsT=kT_[:, kc], rhs=qT_[:, :, qc],
                                                            start=True, stop=True),
                                     reads=[Bq_, Bk_], writes=[BSp], signal=True)
                            else:
                                for hh in range(2):
                                    c.op(pe, lambda e: e.matmul(Sp[:, hh, 0:nq], lhsT=kT_[:, kc], rhs=qT_[:, hh, qc],
                                                                start=True, stop=True),
                                         reads=[Bq_, Bk_], writes=[BSp], signal=(hh == 1))
                            if "noexp" not in KSKIP:
                              c.op(act, lambda e: e.activation(out=Pr[:, :, 0:nq], in_=Sp[:, :, 0:nq], func=AF.Exp, scale=0.125),
                                 reads=[BSp], writes=[BPr])
                            if "nomask" not in KSKIP:
                              c.op(dve, lambda e: e.tensor_tensor(out=Pt[:, :, 0:nq], in0=Pr[:, :, 0:nq],
                                                                 in1=maskAT_bf[:, None, 0:nq].to_broadcast([128, 2, nq]),
                                                                 op=ALU.mult),
                                 reads=[BPr, B_const], writes=[BPt])
                            c.at(u + NU - 0.5)
                            Oc, BOc = psO[j % 2], B_psO[j % 2]
                            if "pv" in KSKIP:
                                u += 1
                                continue
                            c.op(pe, lambda e: e.matmul(Oc[:, 128:256], lhsT=v_aug[:, blk, 68:196], rhs=Pt[:, 1, 0:128],
                                                        start=(j == 0), stop=True, skip_group_check=True),
                                 reads=[B_vd, BPt], writes=[BOc], signal=False)
                            c.op(pe, lambda e: e.matmul(Oc[0:65, 0:128], lhsT=v_aug[:, blk, 0:65], rhs=Pt[:, 0, 0:128],
                                                        start=False, stop=True, skip_group_check=True),
                                 reads=[B_vd, BPt], writes=[BOc], signal=True)
                            if j + 1 < nb:
                                On, BOn = psO[(j + 1) % 2], B_psO[(j + 1) % 2]
                                c.op(pe, lambda e: e.matmul(On[:, 128:256], lhsT=v_aug[:, blk, 68:196], rhs=Pt[:, 1, 128:256],
                                                            start=True, stop=False, skip_group_check=True),
                                     reads=[B_vd, BPt], writes=[BOn], signal=False)
                                c.op(pe, lambda e: e.matmul(On[0:65, 0:128], lhsT=v_aug[:, blk, 0:65], rhs=Pt[:, 0, 128:256],
                                                            start=False, stop=False, skip_group_check=True),
                                     reads=[B_vd, BPt], writes=[BOn], signal=True)
                            oc = cols(r, 128 * j, 128)
                            ov = Oc[:, 0:256].rearrange("p (a i) -> p a i", a=2)
                            if d == 1:
                                c.op(dve, lambda e: e.tensor_copy(out=acc_all[:, :, oc], in_=ov), reads=[BOc], writes=[B_acc])
                            else:
                                c.op(dve, lambda e: e.tensor_tensor(out=acc_all[:, :, oc], in0=ov, in1=acc_all[:, :, oc],
                                                                    op=ALU.add),
                                     reads=[BOc, B_acc], writes=[B_acc])
                            u += 1
                c.at(u + NU - 1.4)
                for cc in range(0 if "norm" in KSKIP else 8):
                    cs_ = slice(cc * 512, (cc + 1) * 512)
                    c.op(pe, lambda e: e.matmul(psR[0:64, :], lhsT=ones_f[64:65, 0:64], rhs=acc_all[64:65, 0, cs_],
                                                start=True, stop=True, skip_group_check=True),
                         reads=[B_acc, B_const], writes=[B_psR], signal=True)
                    c.op(pe, lambda e: e.matmul(psR2[64:128, :], lhsT=ones_f[32:33, 0:64], rhs=acc_all[32:33, 1, cs_],
                                                start=True, stop=True, skip_group_check=True),
                         reads=[B_acc, B_const], writes=[B_psR2], signal=True)
                    c.op(act, lambda e: e.activation(out=rd[0:64, :], in_=psR[0:64, :], func=AF.Ln), reads=[B_psR], writes=[B_rd])
                    c.op(act, lambda e: e.activation(out=rd[64:128, :], in_=psR2[64:128, :], func=AF.Ln),
                         reads=[B_psR2], writes=[B_rd])
                    c.op(act, lambda e: e.activation(out=rd[:], in_=rd[:], func=AF.Exp, scale=-1.0), reads=[B_rd], writes=[B_rd])
                    c.op(dve, lambda e: e.tensor_tensor(out=attnT[0:64, cs_], in0=acc_all[0:64, 0, cs_], in1=rd[0:64, :],
                                                        op=ALU.mult), reads=[B_rd, B_acc], writes=[B_attnT])
                    c.op(dve, lambda e: e.tensor_tensor(out=attnT[64:128, cs_], in0=acc_all[64:128, 1, cs_], in1=rd[64:128, :],
                                                        op=ALU.mult), reads=[B_rd, B_acc], writes=[B_attnT])
                c.dma("sp", catT_s[:, :, hp, :].rearrange("t p i -> p t i"), attnT[:].rearrange("p (t i) -> p t i", i=128),
                      reads=[B_attnT], writes=B_catA_tiles)
            if "norec" not in KSKIP:
                c.flush()
            c.barrier()

        esAB.close()
        if "C" in phases:
          with ExitStack() as ph:
            w_up_bf = sb("w_up_bf", [128, 8, 4096], BF16, ph)
            w_down_bf = sb("w_down_bf", [128, 32, D], BF16, ph)
            B_wup, B_wdn = Buf("wup"), Buf("wdn")
            rl = sb("rl", [128, D], F32, ph)
            tmp = sb("tmp", [128, D], F32, ph)
            B_rl, B_tmp = Buf("rl"), Buf("tmp")
            gpk = sb("gpk", [128, 8], F32, ph)
            B_gpk = Buf("gpk")
            c.dma("sp", gpk[:], g_mlp_pre_pk_d[:, :], writes=[B_gpk])
            catT = sb("catT", [128, 8, 128], BF16, ph)
            xh = [sb(f"xh{i}", [128, D], F32, ph) for i in range(2)]
            B_xh = [Buf("xh0"), Buf("xh1")]
            wvu = w_up_s.rearrange("(k p) n -> p k n", p=128)
            wvd = w_down_s.rearrange("(k p) n -> p k n", p=128)
            B_wupg = [Buf(f"wupg{g}") for g in range(4)]
            B_wdng = [Buf(f"wdng{g}") for g in range(4)]
            engs3 = [dve, act]
            ei = 0
            for g_ in range(4):
                cs_ = slice(g_ * 1024, (g_ + 1) * 1024)
                c.dma("sp", w_up_bf[:, :, cs_], wvu[:, :, cs_], reads=B_wus, writes=[B_wupg[g_]])
                for k4 in (2 * g_, 2 * g_ + 1):
                    c.dma("sp", w_down_bf[:, 4 * k4:4 * k4 + 4, :], wvd[:, 4 * k4:4 * k4 + 4, :], reads=[B_wds[k4]],
                          writes=[B_wdng[g_]])
                for k in range(8):
                    E_ = engs3[ei % 2]
                    ei += 1
                    if E_ is act:
                        c.op(act, lambda e: e.activation(out=w_up_bf[:, k, cs_], in_=w_up_bf[:, k, cs_], func=AF.Copy,
                                                         scale=gpk[:, k:k + 1]),
                             reads=[B_gpk, B_wupg[g_]], writes=[B_wupg[g_]])
                    else:
                        c.op(E_, lambda e: e.tensor_scalar(out=w_up_bf[:, k, cs_], in0=w_up_bf[:, k, cs_],
                                                           scalar1=gpk[:, k:k + 1], scalar2=None, op0=ALU.mult),
                             reads=[B_gpk, B_wupg[g_]], writes=[B_wupg[g_]])
            p_bf = sb("p_bf", [128, 256], BF16, ph)
            pT = sb("pT", [128, 2, 128], BF16, ph)
            u2T = sb("u2T", [128, 8, 128], BF16, ph)
            hT = sb("hT", [128, 8, 128], BF16, ph)
            a_bf = [sb(f"a_bf{i}", [128, D], BF16, ph) for i in range(2)]
            aT = sb("aT", [128, 8, 128], BF16, ph)
            junkC = sb("junkC", [128, 512], BF16, ph)
            stC = sb("stC", [128, 24], F32, ph)
            B_catT, B_pbf, B_pT = Buf("catT"), Buf("pbf"), Buf("pT")
            B_u2T, B_hT, B_aT, B_junkC = Buf("u2T"), Buf("hT"), Buf("aT"), Buf("junkC")
            B_a = [Buf("a0"), Buf("a1")]
            B_st = [Buf("stF"), Buf("stM"), Buf("stP")]
            psU = [[psum(f"psU{s_}{i}", [128, 512], F32, ph) for i in range(2)] for s_ in range(2)]
            B_psU = [[Buf(f"psU{s_}{i}") for i in range(2)] for s_ in range(2)]
            psF = [psum(f"psF{i}", [128, 512], F32, ph) for i in range(2)]
            B_psF = [Buf("psF0"), Buf("psF1")]
            psT2 = psum("psT2", [128, 1024], BF16, ph)
            B_psT2 = Buf("psT2")
            psS = psum("psS_", [128, 512], F32, ph)
            B_psS = Buf("psS")

            def rstd_chain(srcs, si):
                o = 8 * si
                Bs = B_st[si]
                c.op(dve, lambda e: e.memset(stC[:, o:o + 2], 0.0), writes=[Bs])
                for n, (ap_, Bap) in enumerate(srcs):
                    c.op(act, lambda e: e.activation(out=junkC[:], in_=ap_, func=AF.Square, accum_out=stC[:, o + n:o + n + 1]),
                         reads=[Bap], writes=[Bs, B_junkC])
                c.op(dve, lambda e: e.tensor_tensor(out=stC[:, o + 2:o + 3], in0=stC[:, o:o + 1], in1=stC[:, o + 1:o + 2],
                                                    op=ALU.add), reads=[Bs], writes=[Bs])
                c.op(act, lambda e: e.activation(out=stC[:, o + 2:o + 3], in_=stC[:, o + 2:o + 3], func=AF.Ln,
                                                 scale=1.0 / D, bias=EPS), reads=[Bs], writes=[Bs])
                c.op(act, lambda e: e.activation(out=stC[:, o + 3:o + 4], in_=stC[:, o + 2:o + 3], func=AF.Exp, scale=-0.5),
                     reads=[Bs], writes=[Bs])
                return stC[:, o + 3:o + 4], Bs

            def transposesN(src, Bsrc, dstT, BdstT, n=8):
                for k in range(n):
                    c.op(pe, lambda e: e.transpose(psT2[:, k * 128:(k + 1) * 128], src[:, k * 128:(k + 1) * 128], ident_bf[:]),
                         reads=[Bsrc, B_const], writes=[B_psT2], signal=(k == n - 1))
                c.op(act, lambda e: e.copy(out=dstT[:, 0:n, :], in_=psT2[:, 0:n * 128].rearrange("p (k i) -> p k i", k=n)),
                     reads=[B_psT2], writes=[BdstT])

            def proj2(ps2, Bps2, lhsT_t, BlhsT, w_t, Bw, nk, col0=0):
                for n in range(2):
                    for k in range(nk):
                        c.op(pe, lambda e: e.matmul(ps2[n][:], lhsT=lhsT_t[:, k, :],
                                                    rhs=w_t[:, k, col0 + n * 512:col0 + (n + 1) * 512],
                                                    start=(k == 0), stop=(k == nk - 1)),
                             reads=[BlhsT] + (list(Bw) if isinstance(Bw, (list, tuple)) else [Bw]), writes=[Bps2[n]],
                             signal=(k == nk - 1))

            PC = 10.0
            USET = [1, 0, 1, 1]
            c.begin_rec()
            for t in range(NT):
                rows = slice(t * 128, (t + 1) * 128)
                T0 = t * PC
                h, B_h = xh[t % 2], B_xh[t % 2]
                c.at(T0 - 6.5)
                c.dma("sp", catT[:], catT_s[t, :, :, :], reads=[B_catY_tiles[t], B_catA_tiles[t]], writes=[B_catT])
                c.dma("sp", h[:], x_d[rows, :], writes=[B_h])
                c.at(T0 - 3.5)
                proj2(psU[0], B_psU[0], catT, B_catT, w_out_bf, B_wg, 8)
                c.at(T0 - 2.0)
                r_, Br_ = rstd_chain([(psU[0][0][:], B_psU[0][0]), (psU[0][1][:], B_psU[0][1])], 0)
                for n in range(2):
                    hs = slice(n * 512, (n + 1) * 512)
                    c.op(dve, lambda e: e.scalar_tensor_tensor(out=tmp[:, hs], in0=psU[0][n][:], scalar=r_, in1=g_mix_post[:, hs],
                                                               op0=ALU.mult, op1=ALU.mult),
                         reads=[B_psU[0][n], Br_, B_gains], writes=[B_tmp])
                c.op(dve, lambda e: e.tensor_tensor(out=h[:], in0=h[:], in1=tmp[:], op=ALU.add),
                     reads=[B_h, B_tmp], writes=[B_h])
                r_, Br_ = rstd_chain([(h[:, 0:512], B_h), (h[:, 512:1024], B_h)], 0)
                c.op(dve, lambda e: e.tensor_scalar(out=a_bf[0][:], in0=h[:], scalar1=r_, scalar2=None, op0=ALU.mult),
                     reads=[B_h, Br_], writes=[B_a[0]])
                c.at(T0 - 0.5)
                transposesN(a_bf[0], B_a[0], u2T, B_u2T)
                for fg in range(4):
                    c.at(T0 + 2.0 * fg)
                    X, BX = psU[USET[fg]], B_psU[USET[fg]]
                    ab, Bab = a_bf[fg % 2], B_a[fg % 2]
                    proj2(X, BX, u2T, B_u2T, w_up_bf, B_wupg[fg], 8, col0=fg * 1024)
                    for n in range(2):
                        hs = slice(n * 512, (n + 1) * 512)
                        c.op(act, lambda e: e.activation(out=rl[:, hs], in_=X[n][:], func=AF.Relu), reads=[BX[n]], writes=[B_rl])
                    c.op(pool, lambda e: e.tensor_tensor(out=ab[:], in0=rl[:], in1=rl[:], op=ALU.mult),
                         reads=[B_rl], writes=[Bab])
                    c.at(T0 + 1.9 + 2.0 * fg)
                    transposesN(ab, Bab, aT, B_aT)
                    c.at(T0 + 3.0 + 2.0 * fg)
                    for n in range(2):
                        for i in range(8):
                            c.op(pe, lambda e: e.matmul(psF[n][:], lhsT=aT[:, i, :],
                                                        rhs=w_down_bf[:, fg * 8 + i, n * 512:(n + 1) * 512],
                                                        start=(fg == 0 and i == 0), stop=(fg == 3 and i == 7)),
                                 reads=[B_aT, B_wdng[fg]], writes=[B_psF[n]], signal=(i == 7))
                    if fg == 2:
                        c.at(T0 + 4.5)
                        c.dma("pool", p_bf[:], p_d[rows, :], writes=[B_pbf])
                c.at(T0 + 10.0)
                r_, Br_ = rstd_chain([(psF[0][:], B_psF[0]), (psF[1][:], B_psF[1])], 1)
                for n in range(2):
                    hs = slice(n * 512, (n + 1) * 512)
                    c.op(dve, lambda e: e.scalar_tensor_tensor(out=tmp[:, hs], in0=psF[n][:], scalar=r_, in1=g_mlp_post[:, hs],
                                                               op0=ALU.mult, op1=ALU.mult),
                         reads=[B_psF[n], Br_, B_gains], writes=[B_tmp])
                c.op(dve, lambda e: e.tensor_tensor(out=h[:], in0=h[:], in1=tmp[:], op=ALU.add),
                     reads=[B_h, B_tmp], writes=[B_h])
                c.op(dve, lambda e: e.tensor_copy(out=a_bf[1][:], in_=h[:]), reads=[B_h], writes=[B_a[1]])
                c.at(T0 + 10.5)
                transposesN(a_bf[1], B_a[1], hT, B_hT)
                transposesN(p_bf, B_pbf, pT, B_pT, n=2)
                gate_ps = [(psS, B_psS), (psU[1][0], B_psU[1][0])]
                proj_ps = [(psU[1][1], B_psU[1][1]), (psS, B_psS)]
                c.at(T0 + 11.0)
                for n in range(2):
                    hs = slice(n * 512, (n + 1) * 512)
                    gp, Bgp = gate_ps[n]
                    for k in range(8):
                        c.op(pe, lambda e: e.matmul(gp[:], lhsT=hT[:, k, :], rhs=w_gate_bf[:, k, hs], start=(k == 0), stop=(k == 7)),
                             reads=[B_hT, B_wg], writes=[Bgp], signal=(k == 7))
                    c.op(act, lambda e: e.activation(out=tmp[:, hs], in_=gp[:], func=AF.Sigmoid), reads=[Bgp], writes=[B_tmp])
                c.at(T0 + 11.5)
                for n in range(2):
                    hs = slice(n * 512, (n + 1) * 512)
                    pp_, Bpp = proj_ps[n]
                    for k in range(2):
                        c.op(pe, lambda e: e.matmul(pp_[:], lhsT=pT[:, k, :], rhs=w_proj_bf[:, k, hs], start=(k == 0), stop=(k == 1)),
                             reads=[B_pT, B_wg], writes=[Bpp], signal=(k == 1))
                    c.op(dve, lambda e: e.tensor_tensor(out=tmp[:, hs], in0=pp_[:], in1=tmp[:, hs], op=ALU.mult),
                         reads=[Bpp, B_tmp], writes=[B_tmp])
                c.at(T0 + 12.0)
                r_, Br_ = rstd_chain([(tmp[:, 0:512], B_tmp), (tmp[:, 512:1024], B_tmp)], 2)
                for n in range(2):
                    hs = slice(n * 512, (n + 1) * 512)
                    c.op(dve, lambda e: e.scalar_tensor_tensor(out=tmp[:, hs], in0=tmp[:, hs], scalar=r_, in1=g_ple_post[:, hs],
                                                               op0=ALU.mult, op1=ALU.mult),
                         reads=[B_tmp, Br_, B_gains], writes=[B_tmp])
                c.op(dve, lambda e: e.tensor_tensor(out=h[:], in0=h[:], in1=tmp[:], op=ALU.add),
                     reads=[B_h, B_tmp], writes=[B_h])
                c.at(T0 + 13.0)
                c.dma("sp", out_d[rows, :], h[:], reads=[B_h], writes=[B_out])
            c.flush()
            c.barrier()

        if debug:
            c.dma("sp", dbg["qkvT"][:, :, :, :], qkvT_s[:, :, :, :], reads=B_qkv_tiles)
            c.dma("sp", dbg["catT"][:, :, :, :], catT_s[:, :, :, :], reads=B_catY_tiles + B_catA_tiles)
        c.barrier()
        c.check_deadlock()
    return nc


def _prep_inputs(inputs):
    shared = {}
    f = lambda a: np.ascontiguousarray(np.asarray(a, dtype=np.float32))
    shared["norm_mix_pre"] = f(inputs["norm_mix_pre"][0:1])
    shared["norm_mix_post"] = f(inputs["norm_mix_post"][0:1])
    shared["w_in"] = f(inputs["w_in"][0])
    cw = np.asarray(inputs["conv_w"][0], dtype=np.float32)
    shared["conv_w"] = np.ascontiguousarray(cw.reshape(4, 6, 128).transpose(2, 1, 0))
    cb = np.asarray(inputs["conv_b"][0], dtype=np.float32)
    shared["conv_b"] = np.ascontiguousarray(cb.reshape(6, 128).T)
    shared["dt_bias"] = f(inputs["dt_bias"][0:1])
    shared["a_log"] = f(inputs["a_log"][0:1])
    shared["d_skip"] = f(inputs["d_skip"][0:1])
    shared["ssd_norm_g"] = f(inputs["ssd_norm_g"][0:1])
    shared["w_out"] = f(inputs["w_out"][0])
    shared["norm_mlp_pre_pk"] = np.ascontiguousarray(
        np.asarray(inputs["norm_mlp_pre"][0], dtype=np.float32).reshape(8, 128).T)
    shared["norm_mlp_post"] = f(inputs["norm_mlp_post"][0:1])
    shared["w_up"] = f(inputs["w_up"][0])
    shared["w_down"] = f(inputs["w_down"][0])
    shared["w_ple_gate"] = f(inputs["w_ple_gate"][0])
    shared["w_ple_proj"] = f(inputs["w_ple_proj"][0])
    shared["norm_ple_post"] = f(inputs["norm_ple_post"][0:1])
    x = np.asarray(inputs["x"], dtype=np.float32)
    p = np.asarray(inputs["p"], dtype=np.float32)
    pos = np.asarray(inputs["positions"], dtype=np.int32)
    maps = []
    for b in range(x.shape[0]):
        m = dict(shared)
        m["x"] = np.ascontiguousarray(x[b])
        m["p"] = np.ascontiguousarray(p[0, b])
        m["pos"] = np.ascontiguousarray(pos[b].reshape(NT, 128).T)
        maps.append(m)
    return maps


def kernel(**inputs):
    maps = _prep_inputs(inputs)
    nc = build_nc()
    res = run_bass_kernel_spmd(nc, maps, core_ids=list(range(len(maps))))
    return np.stack([r["out"] for r in res.results], axis=0)
```

```python
import os
import numpy as np
from contextlib import ExitStack
import concourse.bass as bass
import concourse.mybir as mybir
from concourse.bass_utils import run_bass_kernel_spmd

F32 = mybir.dt.float32
BF16 = mybir.dt.bfloat16
I32 = mybir.dt.int32
AF = mybir.ActivationFunctionType
ALU = mybir.AluOpType

S = 4096
D = 1024
NT = S // 128
NIN = 2824
NEG = -30000.0
EPS = 1e-6
TWO_PI = 2.0 * np.pi * (1.0 - 2.5e-7)
SAME_ENG_SYNC = os.environ.get("KSES", "1") == "1"


class Sem:
    def __init__(self, h):
        self.h = h
        self.val = 0


class Eng:
    def __init__(self, name, e, sem):
        self.name = name
        self.e = e
        self.sem = sem
        self.seen = {}


class Buf:
    __slots__ = ("w", "r", "name")

    def __init__(self, name=""):
        self.w = None
        self.r = {}
        self.name = name


class _Proxy:
    def __init__(self):
        self.call = None

    def __getattr__(self, name):
        def f(*a, **k):
            self.call = (name, a, k)
            return self
        return f


class Ctx:
    def __init__(self, nc, es):
        self.nc = nc
        self.rec = None
        self.vt = 0.0
        self.log = {}

        def mk(n):
            return Sem(es.enter_context(nc.semaphore(n)))

        self.pe = Eng("pe", nc.tensor, mk("s_pe"))
        self.act = Eng("act", nc.scalar, mk("s_act"))
        self.dve = Eng("dve", nc.vector, mk("s_dve"))
        self.pool = Eng("pool", nc.gpsimd, mk("s_pool"))
        self.sp = Eng("sp", nc.sync, mk("s_sp"))
        self.engs = [self.pe, self.act, self.dve, self.pool, self.sp]
        self.dma_sems = {"sp": [mk(f"d_sp{i}") for i in range(12)],
                         "pool": [mk(f"d_pl{i}") for i in range(6)]}
        self.dma_rr = {"sp": 0, "pool": 0}

    def _wait(self, E, sem, val):
        if val <= 0 or E.seen.get(sem, 0) >= val:
            return
        E.e.wait_ge(sem.h, val)
        E.seen[sem] = val
        self.log.setdefault(E.name, []).append(("w", sem, val))

    def _deps(self, E, reads, writes):
        deps = {}

        def add(ev):
            if ev is None:
                return
            s, v = ev
            if deps.get(s, 0) < v:
                deps[s] = v

        for b in reads:
            add(b.w)
            if b.name.startswith("ps"):
                for s_, ev in b.r.items():
                    if s_ is not E.sem:
                        add(ev)
        for b in writes:
            add(b.w)
            for ev in b.r.values():
                add(ev)
        for s, v in deps.items():
            if s is E.sem and (E is self.pe or not SAME_ENG_SYNC):
                continue
            self._wait(E, s, v)

    def _mark(self, ev, reads, writes):
        s = ev[0]
        for b in reads:
            old = b.r.get(s)
            if old is None or old[1] < ev[1]:
                b.r[s] = ev
        for b in writes:
            b.w = ev
            b.r = {}

    def op(self, E, fn, reads=(), writes=(), signal=True):
        pr = _Proxy()
        fn(pr)
        if self.rec is not None:
            self.rec.append((self.vt, len(self.rec), "op", E, pr.call, list(reads), list(writes), signal))
        else:
            self._emit_op(E, pr.call, reads, writes, signal)

    def _emit_op(self, E, call, reads, writes, signal):
        self._deps(E, reads, writes)
        name, a, k = call
        ins = getattr(E.e, name)(*a, **k)
        if signal:
            E.sem.val += 1
            ins.then_inc(E.sem.h, 1)
            ev = (E.sem, E.sem.val)
            self.log.setdefault(E.name, []).append(("i", E.sem, 1, name))
        else:
            assert E is self.pe
            ev = (E.sem, E.sem.val + 1)
        self._mark(ev, reads, writes)

    def dma(self, q, out, in_, reads=(), writes=(), **kw):
        if self.rec is not None:
            self.rec.append((self.vt, len(self.rec), "dma", q, out, in_, list(reads), list(writes), kw))
        else:
            self._emit_dma(q, out, in_, reads, writes, kw)

    def _emit_dma(self, q, out, in_, reads, writes, kw):
        E = self.sp if q == "sp" else self.pool
        self._deps(E, reads, writes)
        sems = self.dma_sems[q]
        i = self.dma_rr[q]
        self.dma_rr[q] = (i + 1) % len(sems)
        s = sems[i]
        self._wait(E, s, s.val)
        s.val += 16
        E.e.dma_start(out=out, in_=in_, **kw).then_inc(s.h, 16)
        self.log.setdefault(E.name, []).append(("i", s, 16, "dma"))
        self._mark((s, s.val), reads, writes)

    def check_deadlock(self):
        vals = {}
        pos = {k: 0 for k in self.log}
        progress = True
        while progress:
            progress = False
            for k, lst in self.log.items():
                while pos[k] < len(lst):
                    it = lst[pos[k]]
                    if it[0] == "w":
                        if vals.get(it[1], 0) >= it[2]:
                            pos[k] += 1
                            progress = True
                        else:
                            break
                    else:
                        vals[it[1]] = vals.get(it[1], 0) + it[2]
                        pos[k] += 1
                        progress = True
        bad = {k: (pos[k], len(lst)) for k, lst in self.log.items() if pos[k] < len(lst)}
        if bad:
            msg = []
            names = {}
            for e in self.engs:
                names[e.sem] = "sem_" + e.name
            for q, ss in self.dma_sems.items():
                for i, s_ in enumerate(ss):
                    names[s_] = f"dma_{q}{i}"
            for k, (p_, n_) in bad.items():
                it = self.log[k][p_]
                msg.append(f"{k} blocked at {p_}/{n_}: wait {names.get(it[1])} >= {it[2]} (have {vals.get(it[1], 0)})")
            raise RuntimeError("DEADLOCK: " + "; ".join(msg))

    def begin_rec(self):
        self.rec = []
        self.vt = 0.0

    def at(self, vt):
        self.vt = vt

    def flush(self):
        rec = self.rec
        self.rec = None
        rec.sort(key=lambda r: (r[0], r[1]))
        for r in rec:
            if r[2] == "op":
                self._emit_op(*r[3:])
            else:
                self._emit_dma(*r[3:])

    def barrier(self):
        allsems = [e.sem for e in self.engs] + self.dma_sems["sp"] + self.dma_sems["pool"]
        for E in self.engs:
            for s in allsems:
                if s is E.sem:
                    continue
                self._wait(E, s, s.val)


def build_nc(debug=False, phases="ABC"):
    KSKIP = os.environ.get("KSKIP", "").split(",")
    nc = bass.Bass("TRN2", target_bir_lowering=False)

    def din(name, shape, dt=F32):
        return nc.dram_tensor(name, list(shape), dt, kind="ExternalInput").ap()

    x_d = din("x", [S, D])
    p_d = din("p", [S, 256])
    pos_d = din("pos", [128, NT], I32)
    g_mix_pre_d = din("norm_mix_pre", [1, D])
    g_mix_post_d = din("norm_mix_post", [1, D])
    w_in_d = din("w_in", [D, NIN])
    conv_w_d = din("conv_w", [128, 6, 4])
    conv_b_d = din("conv_b", [128, 6])
    dt_bias_d = din("dt_bias", [1, 8])
    a_log_d = din("a_log", [1, 8])
    d_skip_d = din("d_skip", [1, 8])
    g_ssd_d = din("ssd_norm_g", [1, 512])
    w_out_d = din("w_out", [D, D])
    g_mlp_pre_pk_d = din("norm_mlp_pre_pk", [128, 8])
    g_mlp_post_d = din("norm_mlp_post", [1, D])
    w_up_d = din("w_up", [D, 4096])
    w_down_d = din("w_down", [4096, D])
    w_gate_d = din("w_ple_gate", [D, D])
    w_proj_d = din("w_ple_proj", [256, D])
    g_ple_post_d = din("norm_ple_post", [1, D])
    out_d = nc.dram_tensor("out", [S, D], F32, kind="ExternalOutput").ap()

    qkvT_s = nc.dram_tensor("qkvT_s", [3, 4, 128, S], BF16, kind="Internal").ap()
    catT_s = nc.dram_tensor("catT_s", [NT, 128, 8, 128], BF16, kind="Internal").ap()
    w_up_s = nc.dram_tensor("w_up_s", [D, 4096], BF16, kind="Internal").ap()
    w_down_s = nc.dram_tensor("w_down_s", [4096, D], BF16, kind="Internal").ap()
    dbg = {}
    if debug:
        dbg["qkvT"] = nc.dram_tensor("dbg_qkvT", [3, 4, 128, S], BF16, kind="ExternalOutput").ap()
        dbg["catT"] = nc.dram_tensor("dbg_catT", [NT, 128, 8, 128], BF16, kind="ExternalOutput").ap()

    with ExitStack() as es:
        c = Ctx(nc, es)
        pe, act, dve, pool = c.pe, c.act, c.dve, c.pool

        def sb(name, shape, dt, stack=es):
            return stack.enter_context(nc.sbuf_tensor(name, list(shape), dt))

        def psum(name, shape, dt, stack=es):
            return stack.enter_context(nc.psum_tensor(name, list(shape), dt))

        ident_bf = sb("ident_bf", [128, 128], BF16)
        w_out_bf = sb("w_out_bf", [128, 8, D], BF16)
        w_gate_bf = sb("w_gate_bf", [128, 8, D], BF16)
        w_proj_bf = sb("w_proj_bf", [128, 2, D], BF16)
        g_mix_post = sb("g_mix_post", [128, D], F32)
        g_mlp_post = sb("g_mlp_post", [128, D], F32)
        g_ple_post = sb("g_ple_post", [128, D], F32)
        esAB = ExitStack()
        es.enter_context(esAB)
        cF = sb("cF", [128, 128], F32, esAB)
        tri_f = sb("tri_f", [128, 128], F32, esAB)
        sl_f = sb("sl_f", [128, 128], F32, esAB)
        ones_f = sb("ones_f", [128, 128], F32, esAB)
        ssdmask_bf = sb("ssdmask_bf", [128, 128], BF16, esAB)
        maskAT_bf = sb("maskAT_bf", [128, 256], BF16, esAB)
        maskAT_f = sb("maskAT_f", [128, 256], F32, esAB)
        sel2_f = sb("sel2_f", [2, 128], F32, esAB)
        B_const = Buf("const")

        def pconst(fn, rd=True):
            c.op(pool, fn, reads=[B_const] if rd else [], writes=[B_const])

        pconst(lambda e: e.memset(cF[:], 1.0))
        pconst(lambda e: e.affine_select(out=cF[:], in_=cF[:], pattern=[[-1, 128]], compare_op=ALU.is_equal,
                                         fill=0.0, base=0, channel_multiplier=1))
        pconst(lambda e: e.tensor_copy(out=ident_bf[:], in_=cF[:]))
        pconst(lambda e: e.memset(tri_f[:], 1.0))
        pconst(lambda e: e.affine_select(out=tri_f[:], in_=tri_f[:], pattern=[[1, 128]], compare_op=ALU.is_ge,
                                         fill=0.0, base=0, channel_multiplier=-1))
        pconst(lambda e: e.memset(sl_f[:], 1.0))
        pconst(lambda e: e.affine_select(out=sl_f[:], in_=sl_f[:], pattern=[[-1, 128]], compare_op=ALU.is_gt,
                                         fill=0.0, base=0, channel_multiplier=1))
        pconst(lambda e: e.memset(ones_f[:], 1.0))
        pconst(lambda e: e.memset(cF[:], 0.0))
        pconst(lambda e: e.affine_select(out=cF[:], in_=cF[:], pattern=[[1, 128]], compare_op=ALU.is_ge,
                                         fill=NEG, base=0, channel_multiplier=-1))
        pconst(lambda e: e.tensor_copy(out=ssdmask_bf[:], in_=cF[:]))
        pconst(lambda e: e.memset(maskAT_f[:], 1.0))
        pconst(lambda e: e.affine_select(out=maskAT_f[:], in_=maskAT_f[:], pattern=[[1, 256]], compare_op=ALU.is_ge,
                                         fill=0.0, base=0, channel_multiplier=-1))
        pconst(lambda e: e.affine_select(out=maskAT_f[:], in_=maskAT_f[:], pattern=[[-1, 256]], compare_op=ALU.is_ge,
                                         fill=0.0, base=128, channel_multiplier=1))
        pconst(lambda e: e.tensor_copy(out=maskAT_bf[:], in_=maskAT_f[:]))
        pconst(lambda e: e.memset(sel2_f[:], 1.0))
        pconst(lambda e: e.affine_select(out=sel2_f[:, 0:64], in_=sel2_f[:, 0:64], pattern=[[0, 64]], compare_op=ALU.is_ge,
                                         fill=0.0, base=0, channel_multiplier=-1))
        pconst(lambda e: e.affine_select(out=sel2_f[:, 64:128], in_=sel2_f[:, 64:128], pattern=[[0, 64]],
                                         compare_op=ALU.is_ge, fill=0.0, base=-1, channel_multiplier=1))

        def bcast_load(dst, src_row, n, buf):
            c.dma("sp", dst[:, 0:n], src_row[0:1, 0:n].partition_broadcast(128), writes=[buf])


        B_wg = Buf("wg")
        B_gains = Buf("gains")
        B_out = Buf("out")
        def load_small_weights():
            for wt, wd, nk in ((w_out_bf, w_out_d, 8), (w_gate_bf, w_gate_d, 8), (w_proj_bf, w_proj_d, 2)):
                wv_ = wd.rearrange("(k p) n -> p k n", p=128)
                for k0 in range(0, nk, 4):
                    k1 = min(nk, k0 + 4)
                    c.dma("pool", wt[:, k0:k1, :], wv_[:, k0:k1, :], writes=[B_wg])

        if "A" not in phases:
            load_small_weights()
        for gt, gd in ((g_mix_post, g_mix_post_d), (g_mlp_post, g_mlp_post_d), (g_ple_post, g_ple_post_d)):
            bcast_load(gt, gd, D, B_gains)

        B_wus = [Buf(f"wus{k}") for k in range(8)]
        B_wds = [Buf(f"wds{k}") for k in range(8)]
        B_qkv_tiles = [Buf(f"qkvs{t}") for t in range(NT)]
        B_catY_tiles = [Buf(f"catY{t}") for t in range(NT)]
        B_catA_tiles = [Buf(f"catA{t}") for t in range(NT)]
        if "A" in phases:
          with ExitStack() as ph:
            w_in_bf = sb("w_in_bf", [128, 8, NIN], BF16, ph)
            B_win = Buf("w_in")
            w_in_v = w_in_d.rearrange("(k p) n -> p k n", p=128)
            for k in range(8):
                c.dma("pool", w_in_bf[:, k, :], w_in_v[:, k, :], writes=[B_win])
            load_small_weights()
            gpre = sb("gpre", [128, D], F32, ph)
            gssd = sb("gssd", [128, 512], F32, ph)
            convw = sb("convw", [128, 6, 4], F32, ph)
            convb = sb("convb", [128, 6], F32, ph)
            sm = sb("sm", [128, 64], F32, ph)
            B_par = Buf("params")
            bcast_load(gpre, g_mix_pre_d, D, B_par)
            bcast_load(gssd, g_ssd_d, 512, B_par)
            c.dma("sp", convw[:], conv_w_d[:, :, :], writes=[B_par])
            c.dma("sp", convb[:], conv_b_d[:, :], writes=[B_par])
            c.dma("sp", sm[:, 0:8], dt_bias_d[0:1, :].partition_broadcast(128), writes=[B_par])
            c.dma("sp", sm[:, 8:16], a_log_d[0:1, :].partition_broadcast(128), writes=[B_par])
            c.dma("sp", sm[:, 16:24], d_skip_d[0:1, :].partition_broadcast(128), writes=[B_par])
            c.op(act, lambda e: e.activation(out=sm[:, 24:32], in_=sm[:, 8:16], func=AF.Exp),
                 reads=[B_par], writes=[B_par])
            c.op(dve, lambda e: e.tensor_scalar(out=sm[:, 24:32], in0=sm[:, 24:32], scalar1=-1.0, scalar2=None,
                                               op0=ALU.mult), reads=[B_par], writes=[B_par])
            dtb_bc, dsk_bc, A_bc = sm[:, 0:8], sm[:, 16:24], sm[:, 24:32]

            pos_i = sb("pos_i", [128, NT], I32, ph)
            posf = sb("posf", [128, NT], F32, ph)
            invf = sb("invf", [128, 32], F32, ph)
            X = sb("ropeX", [128, NT, 32], F32, ph)
            Xc = sb("ropeXc", [128, NT, 32], F32, ph)
            Ki = sb("ropeKi", [128, NT, 32], I32, ph)
            Kf = sb("ropeKf", [128, NT, 32], F32, ph)
            cos2 = sb("cos2", [128, NT, 64], F32, ph)
            sin2 = sb("sin2", [128, NT, 64], F32, ph)
            B_rope = Buf("rope")
            c.dma("sp", pos_i[:], pos_d[:, :], writes=[B_rope])
            invf_lo = sb("invf_lo", [128, 32], F32, ph)
            for j in range(32):
                val = 10000.0 ** (-(2.0 * j) / 64.0) / (2.0 * np.pi)
                hi_ = float(np.float32(val))
                lo_ = float(np.float32(val - float(np.float32(val))))
                c.op(pool, lambda e, j=j, v_=hi_: e.memset(invf[:, j:j + 1], v_), writes=[B_rope])
                c.op(pool, lambda e, j=j, v_=lo_: e.memset(invf_lo[:, j:j + 1], v_), writes=[B_rope])
            c.op(dve, lambda e: e.tensor_copy(out=posf[:], in_=pos_i[:]), reads=[B_rope], writes=[B_rope])
            c.op(dve, lambda e: e.tensor_tensor(out=X[:], in0=posf[:, :, None].to_broadcast([128, NT, 32]),
                                                in1=invf[:, None, :].to_broadcast([128, NT, 32]), op=ALU.mult),
                 reads=[B_rope], writes=[B_rope])
            c.op(dve, lambda e: e.tensor_tensor(out=Xc[:], in0=posf[:, :, None].to_broadcast([128, NT, 32]),
                                                in1=invf_lo[:, None, :].to_broadcast([128, NT, 32]), op=ALU.mult),
                 reads=[B_rope], writes=[B_rope])
            c.op(dve, lambda e: e.tensor_tensor(out=X[:], in0=X[:], in1=Xc[:], op=ALU.add),
                 reads=[B_rope], writes=[B_rope])
            c.op(dve, lambda e: e.tensor_copy(out=Ki[:], in_=X[:]), reads=[B_rope], writes=[B_rope])
            c.op(dve, lambda e: e.tensor_copy(out=Kf[:], in_=Ki[:]), reads=[B_rope], writes=[B_rope])
            c.op(dve, lambda e: e.tensor_tensor(out=Kf[:], in0=X[:], in1=Kf[:], op=ALU.subtract),
                 reads=[B_rope], writes=[B_rope])
            c.op(act, lambda e: e.activation(out=sin2[:, :, 32:64], in_=Kf[:], func=AF.Sin, scale=TWO_PI),
                 reads=[B_rope], writes=[B_rope])
            c.op(act, lambda e: e.activation(out=sin2[:, :, 0:32], in_=Kf[:], func=AF.Sin, scale=-TWO_PI),
                 reads=[B_rope], writes=[B_rope])
            c.op(dve, lambda e: e.tensor_scalar(out=Xc[:], in0=X[:], scalar1=0.25, scalar2=None, op0=ALU.add),
                 reads=[B_rope], writes=[B_rope])
            c.op(dve, lambda e: e.tensor_copy(out=Ki[:], in_=Xc[:]), reads=[B_rope], writes=[B_rope])
            c.op(dve, lambda e: e.tensor_copy(out=Kf[:], in_=Ki[:]), reads=[B_rope], writes=[B_rope])
            c.op(dve, lambda e: e.tensor_tensor(out=Kf[:], in0=Xc[:], in1=Kf[:], op=ALU.subtract),
                 reads=[B_rope], writes=[B_rope])
            c.op(act, lambda e: e.activation(out=cos2[:, :, 0:32], in_=Kf[:], func=AF.Sin, scale=TWO_PI),
                 reads=[B_rope], writes=[B_rope])
            c.op(act, lambda e: e.activation(out=cos2[:, :, 32:64], in_=Kf[:], func=AF.Sin, scale=TWO_PI),
                 reads=[B_rope], writes=[B_rope])

            xt = [sb(f"xt{i}", [128, D], F32, ph) for i in range(2)]
            B_xt = [Buf("xt0"), Buf("xt1")]
            junk = sb("junk", [128, D], BF16, ph)
            junkB = sb("junkB", [128, 512], BF16, ph)
            B_junk, B_junkB = Buf("junk"), Buf("junkB")
            stF = sb("statF", [128, 8], F32, ph)
            stB = sb("statB", [128, 8], F32, ph)
            B_stF, B_stB = Buf("statF"), Buf("statB")
            u_bf = sb("u_bf", [128, D], BF16, ph)
            B_u = Buf("u")
            uT = sb("uT", [128, 8, 128], BF16, ph)
            B_uT = Buf("uT")
            tA = sb("tA", [128, 512], F32, ph)
            tB = sb("tB", [128, 512], F32, ph)
            B_tA, B_tB = Buf("tA"), Buf("tB")
            qkv_bf = sb("qkv_bf", [128, 3, 512], BF16, ph)
            B_qkv = [Buf("q_bf"), Buf("k_bf"), Buf("v_bf")]
            T3 = sb("T3", [128, 3, 4, 128], BF16, ph)
            B_T3 = [Buf("T3q"), Buf("T3k"), Buf("T3v")]
            zs2 = [sb(f"zs{i}", [128, 512], F32, ph) for i in range(2)]
            B_zs2 = [Buf("zs0"), Buf("zs1")]
            pre = sb("pre", [128, 6, 132], BF16, ph)
            B_pre = Buf("pre")
            Wd = sb("Wd", [128, 6, 4, 128], BF16, ph)
            B_Wd = Buf("Wd")
            for cc in range(6):
                for j in range(4):
                    c.op(dve, lambda e: e.tensor_scalar(out=Wd[:, cc, j, :], in0=ident_bf[:], scalar1=convw[:, cc, j:j + 1],
                                                        scalar2=None, op0=ALU.mult),
                         reads=[B_const, B_par], writes=[B_Wd])
            cacc = sb("cacc", [128, 6, 128], F32, ph)
            B_cacc = Buf("cacc")
            ctmp = sb("ctmp", [128, 6, 128], F32, ph)
            B_ctmp = Buf("ctmp")
            xbcT2 = [sb(f"xbcT_bf{i}", [128, 6, 128], BF16, ph) for i in range(2)]
            B_xbcT2 = [Buf("xbcT0"), Buf("xbcT1")]
            dtraw2 = [sb(f"dtraw{i}", [128, 8], F32, ph) for i in range(2)]
            B_dtraw2 = [Buf("dtraw0"), Buf("dtraw1")]
            xsB = sb("xsB", [128, 5, 128], BF16, ph)
            B_xsB = Buf("xsB")
            dtt = sb("dtt", [128, 64], F32, ph)
            B_dtt = Buf("dtt")
            L_all = sb("L_all", [128, 8, 128], F32, ph)
            B_L = Buf("L")
            decT = sb("decT", [128, 8, 128], F32, ph)
            B_dec = Buf("dec")
            GT = sb("GT", [128, 8, 128], BF16, ph)
            B_GT = Buf("GT")
            xdt = sb("xdt", [128, 512], BF16, ph)
            xdtd = sb("xdtd", [128, 512], BF16, ph)
            B_xdt, B_xdtd = Buf("xdt"), Buf("xdtd")
            state = sb("state", [128, 512], F32, ph)
            state_bf = sb("state_bf", [128, 512], BF16, ph)
            B_state, B_statebf = Buf("state"), Buf("statebf")
            y1 = sb("y1", [128, 512], F32, ph)
            y2 = sb("y2", [128, 512], F32, ph)
            B_y1, B_y2 = Buf("y1"), Buf("y2")
            yn_bf = sb("yn_bf", [128, 512], BF16, ph)
            B_yn = Buf("yn")
            yT_st = sb("yT_st", [128, 4, 128], BF16, ph)
            B_yT = Buf("yT")

            psT = psum("psT", [128, 1024], BF16, ph)
            psP = [psum(f"psP{i}", [128, 512], F32, ph) for i in range(4)]
            psW = [psum(f"psW{i}", [128, 512], F32, ph) for i in range(3)]
            B_psT = Buf("psT")
            B_psP = [Buf(f"psP{i}") for i in range(4)]
            B_psW = [Buf(f"psW{i}") for i in range(3)]

            c.op(pool, lambda e: e.memset(pre[:, :, 0:3], 0.0), writes=[B_pre])
            c.op(dve, lambda e: e.memset(state[:], 0.0), writes=[B_state])
            c.op(dve, lambda e: e.memset(state_bf[:], 0.0), writes=[B_statebf])

            PA = float(os.environ.get("KPA", "2.45"))
            if "noreca" not in KSKIP:
                c.begin_rec()

            def front(t):
                T0 = t * PA
                xb, Bx = xt[t % 2], B_xt[t % 2]
                zs, B_zs = zs2[t % 2], B_zs2[t % 2]
                xbcT_bf, B_xbcT = xbcT2[t % 2], B_xbcT2[t % 2]
                dtraw, B_dtraw = dtraw2[t % 2], B_dtraw2[t % 2]
                c.at(T0 - 3.9)
                c.dma("sp", xb[:], x_d[t * 128:(t + 1) * 128, :], writes=[Bx])
                c.at(T0 - 2.4)
                c.op(dve, lambda e: e.memset(stF[:, 0:1], 0.0), writes=[B_stF])
                c.op(act, lambda e: e.activation(out=junk[:], in_=xb[:], func=AF.Square, accum_out=stF[:, 0:1]),
                     reads=[Bx], writes=[B_stF, B_junk])
                c.op(act, lambda e: e.activation(out=stF[:, 1:2], in_=stF[:, 0:1], func=AF.Ln, scale=1.0 / D, bias=EPS),
                     reads=[B_stF], writes=[B_stF])
                c.op(act, lambda e: e.activation(out=stF[:, 2:3], in_=stF[:, 1:2], func=AF.Exp, scale=-0.5),
                     reads=[B_stF], writes=[B_stF])
                c.op(dve, lambda e: e.scalar_tensor_tensor(out=u_bf[:], in0=xb[:], scalar=stF[:, 2:3], in1=gpre[:],
                                                           op0=ALU.mult, op1=ALU.mult),
                     reads=[Bx, B_stF, B_par], writes=[B_u])
                c.at(T0 + 0.0)
                for k in range(8):
                    c.op(pe, lambda e: e.transpose(psT[:, k * 128:(k + 1) * 128], u_bf[:, k * 128:(k + 1) * 128], ident_bf[:]),
                         reads=[B_u, B_const], writes=[B_psT], signal=(k == 7))
                c.op(act, lambda e: e.copy(out=uT[:], in_=psT[:].rearrange("p (k i) -> p k i", k=8)),
                     reads=[B_psT], writes=[B_uT])

                def inproj(pt, Bp, col0, vt):
                    c.at(vt)
                    for k in range(8):
                        c.op(pe, lambda e: e.matmul(pt[:], lhsT=uT[:, k, :], rhs=w_in_bf[:, k, col0:col0 + 512],
                                                    start=(k == 0), stop=(k == 7)),
                             reads=[B_uT, B_win], writes=[Bp], signal=(k == 7))

                inproj(psP[0], B_psP[0], 0, T0 + 0.3)
                inproj(psP[1], B_psP[1], 512, T0 + 0.3)
                c.at(T0 + 0.5)
                for qi in range(2):
                    pt, Bp = psP[qi], B_psP[qi]
                    pv = pt[:].rearrange("p (h d) -> p h d", h=8)
                    c.op(dve, lambda e: e.tensor_tensor(out=tA[:].rearrange("p (h d) -> p h d", h=8), in0=pv,
                                                        in1=cos2[:, t, None, :].to_broadcast([128, 8, 64]), op=ALU.mult),
                         reads=[Bp, B_rope], writes=[B_tA])
                    tBv = tB[:].rearrange("p (h d) -> p h d", h=8)
                    c.op(dve, lambda e: e.tensor_tensor(out=tBv[:, :, 0:32], in0=pv[:, :, 32:64],
                                                        in1=sin2[:, t, None, 0:32].to_broadcast([128, 8, 32]), op=ALU.mult),
                         reads=[Bp, B_rope], writes=[B_tB])
                    c.op(dve, lambda e: e.tensor_tensor(out=tBv[:, :, 32:64], in0=pv[:, :, 0:32],
                                                        in1=sin2[:, t, None, 32:64].to_broadcast([128, 8, 32]), op=ALU.mult),
                         reads=[Bp, B_rope], writes=[B_tB])
                    c.op(dve, lambda e: e.tensor_tensor(out=qkv_bf[:, qi, :], in0=tA[:], in1=tB[:], op=ALU.add),
                         reads=[B_tA, B_tB], writes=[B_qkv[qi]])
                inproj(psP[2], B_psP[2], 1024, T0 + 0.6)
                inproj(psP[3], B_psP[3], 1536, T0 + 0.6)
                c.at(T0 + 0.8)
                c.op(act, lambda e: e.copy(out=qkv_bf[:, 2, :], in_=psP[2][:]), reads=[B_psP[2]], writes=[B_qkv[2]])
                c.op(act, lambda e: e.activation(out=zs[:], in_=psP[3][:], func=AF.Silu), reads=[B_psP[3]], writes=[B_zs])
                c.at(T0 + 1.0)
                for k in range(8):
                    c.op(pe, lambda e: e.matmul(psP[1][:, 256:264], lhsT=uT[:, k, :], rhs=w_in_bf[:, k, 2816:2824],
                                                start=(k == 0), stop=(k == 7)),
                         reads=[B_uT, B_win], writes=[B_psP[1]], signal=False)
                for cc in range(6):
                    for k in range(8):
                        dst = psP[0][:, cc * 128:(cc + 1) * 128] if cc < 4 else psP[1][:, (cc - 4) * 128:(cc - 3) * 128]
                        c.op(pe, lambda e: e.matmul(dst, lhsT=w_in_bf[:, k, 2048 + cc * 128:2048 + (cc + 1) * 128],
                                                    rhs=uT[:, k, :], start=(k == 0), stop=(k == 7)),
                             reads=[B_uT, B_win], writes=[B_psP[0] if cc < 4 else B_psP[1]],
                             signal=((cc == 3 or cc == 5) and k == 7))
                c.at(T0 + 1.3)
                c.op(act, lambda e: e.copy(out=pre[:, 0:4, 3:131], in_=psP[0][:].rearrange("p (k i) -> p k i", k=4)),
                     reads=[B_psP[0]], writes=[B_pre])
                c.op(act, lambda e: e.copy(out=pre[:, 4:6, 3:131], in_=psP[1][:, 0:256].rearrange("p (k i) -> p k i", k=2)),
                     reads=[B_psP[1]], writes=[B_pre])
                c.op(dve, lambda e: e.tensor_tensor(out=dtraw[:], in0=psP[1][:, 256:264], in1=dtb_bc, op=ALU.add),
                     reads=[B_psP[1], B_par], writes=[B_dtraw])
                for cc in range(6):
                    dstc = psP[2][:, cc * 128:(cc + 1) * 128] if cc < 4 else psP[3][:, (cc - 4) * 128:(cc - 3) * 128]
                    for j in range(4):
                        c.op(pe, lambda e: e.matmul(dstc, lhsT=Wd[:, cc, j, :], rhs=pre[:, cc, j:j + 128],
                                                    start=(j == 0), stop=(j == 3)),
                             reads=[B_pre, B_Wd], writes=[B_psP[2] if cc < 4 else B_psP[3]],
                             signal=((cc == 3 or cc == 5) and j == 3))
                c.op(pool, lambda e: e.tensor_copy(out=pre[:, :, 0:3], in_=pre[:, :, 128:131]),
                     reads=[B_pre], writes=[B_pre])
                c.at(T0 + 1.6)
                for qi in range(3):
                    for hp in range(4):
                        c.op(pe, lambda e: e.transpose(psT[:, hp * 128:(hp + 1) * 128], qkv_bf[:, qi, hp * 128:(hp + 1) * 128],
                                                       ident_bf[:]),
                             reads=[B_qkv[qi], B_const], writes=[B_psT], signal=(hp == 3))
                    c.op(act, lambda e: e.copy(out=T3[:, qi, :, :], in_=psT[:, 0:512].rearrange("p (k i) -> p k i", k=4)),
                         reads=[B_psT], writes=[B_T3[qi]])
                    c.dma("sp", qkvT_s[qi, :, :, t * 128:(t + 1) * 128].rearrange("h p i -> p h i"), T3[:, qi, :, :],
                          reads=[B_T3[qi]], writes=[B_qkv_tiles[t]])
                c.at(T0 + 2.0)
                for cc in range(6):
                    srcc = psP[2][:, cc * 128:(cc + 1) * 128] if cc < 4 else psP[3][:, (cc - 4) * 128:(cc - 3) * 128]
                    c.op(act, lambda e: e.activation(out=xbcT_bf[:, cc, :], in_=srcc, func=AF.Silu, bias=convb[:, cc:cc + 1]),
                         reads=[B_psP[2] if cc < 4 else B_psP[3], B_par], writes=[B_xbcT])

            def back(t):
                T0 = t * PA + 3.0
                zs, B_zs = zs2[t % 2], B_zs2[t % 2]
                xbcT_bf, B_xbcT = xbcT2[t % 2], B_xbcT2[t % 2]
                dtraw, B_dtraw = dtraw2[t % 2], B_dtraw2[t % 2]
                BT = xbcT_bf[:, 4, :]
                CT = xbcT_bf[:, 5, :]
                xs_tok = xsB[:, 0:4, :].rearrange("p k (a d) -> p (k a) d", d=64)
                B_tok = xsB[:, 4, :]
                c.at(T0 + 0.0)
                c.op(act, lambda e: e.activation(out=dtt[:, 48:56], in_=dtraw[:], func=AF.Exp), reads=[B_dtraw], writes=[B_dtt])
                c.op(act, lambda e: e.activation(out=dtt[:, 0:8], in_=dtt[:, 48:56], func=AF.Ln, bias=1.0),
                     reads=[B_dtt], writes=[B_dtt])
                c.op(dve, lambda e: e.tensor_tensor(out=dtt[:, 8:16], in0=dtt[:, 0:8], in1=A_bc, op=ALU.mult),
                     reads=[B_dtt, B_par], writes=[B_dtt])
                for i, m in enumerate((tri_f, sl_f, ones_f)):
                    c.op(pe, lambda e: e.matmul(psW[2][:, 8 * i:8 + 8 * i], lhsT=m[:], rhs=dtt[:, 8:16], start=True, stop=True),
                         reads=[B_dtt, B_const], writes=[B_psW[2]], signal=(i == 2))
                c.op(act, lambda e: e.activation(out=dtt[:, 16:40], in_=psW[2][:, 0:24], func=AF.Exp),
                     reads=[B_psW[2]], writes=[B_dtt])
                ecs, dstate, cdec = dtt[:, 16:24], dtt[:, 24:32], dtt[:, 32:40]
                c.op(dve, lambda e: e.tensor_tensor(out=dtt[:, 40:48], in0=dtt[:, 0:8], in1=dstate, op=ALU.mult),
                     reads=[B_dtt], writes=[B_dtt])
                c.op(pool, lambda e: e.tensor_tensor(out=L_all[:], in0=sl_f[:, None, :].to_broadcast([128, 8, 128]),
                                                     in1=dtt[:, 8:16, None].to_broadcast([128, 8, 128]), op=ALU.mult),
                     reads=[B_dtt, B_const], writes=[B_L])
                c.at(T0 + 0.2)
                for cc in range(5):
                    c.op(pe, lambda e: e.transpose(psT[:, cc * 128:(cc + 1) * 128], xbcT_bf[:, cc, :], ident_bf[:]),
                         reads=[B_xbcT, B_const], writes=[B_psT], signal=(cc == 4))
                c.op(act, lambda e: e.copy(out=xsB[:], in_=psT[:, 0:640].rearrange("p (k i) -> p k i", k=5)),
                     reads=[B_psT], writes=[B_xsB])
                c.at(T0 + 0.4)
                for h in range(8):
                    pd, Bpd = psW[h // 4], B_psW[h // 4]
                    dst = pd[:, (h % 4) * 128:(h % 4 + 1) * 128]
                    c.op(pe, lambda e: e.matmul(dst, lhsT=L_all[:, h, :], rhs=tri_f[:], start=True, stop=False),
                         reads=[B_L, B_const], writes=[Bpd], signal=False)
                    c.op(pe, lambda e: e.matmul(dst, lhsT=ident_bf[:], rhs=ssdmask_bf[:], start=False, stop=True),
                         reads=[B_const], writes=[Bpd], signal=(h % 4 == 3))
                c.op(act, lambda e: e.activation(out=decT[:, 0:4, :], in_=psW[0][:].rearrange("p (h l) -> p h l", h=4),
                                                 func=AF.Exp), reads=[B_psW[0]], writes=[B_dec])
                c.op(act, lambda e: e.activation(out=decT[:, 4:8, :], in_=psW[1][:].rearrange("p (h l) -> p h l", h=4),
                                                 func=AF.Exp), reads=[B_psW[1]], writes=[B_dec])
                c.op(pe, lambda e: e.matmul(psW[2][:, 128:256], lhsT=BT, rhs=CT, start=True, stop=True),
                     reads=[B_xbcT], writes=[B_psW[2]])
                c.op(dve, lambda e: e.tensor_tensor(out=GT[:], in0=psW[2][:, None, 128:256].to_broadcast([128, 8, 128]),
                                                    in1=decT[:], op=ALU.mult),
                     reads=[B_psW[2], B_dec], writes=[B_GT])
                c.op(pool, lambda e: e.tensor_tensor(out=xdt[:].rearrange("p (h d) -> p h d", h=8), in0=xs_tok,
                                                     in1=dtt[:, 0:8, None].to_broadcast([128, 8, 64]), op=ALU.mult),
                     reads=[B_xsB, B_dtt], writes=[B_xdt])
                c.op(pool, lambda e: e.tensor_tensor(out=xdtd[:].rearrange("p (h d) -> p h d", h=8), in0=xs_tok,
                                                     in1=dtt[:, 40:48, None].to_broadcast([128, 8, 64]), op=ALU.mult),
                     reads=[B_xsB, B_dtt], writes=[B_xdtd])
                c.op(pool, lambda e: e.tensor_tensor(out=y2[:].rearrange("p (h d) -> p h d", h=8), in0=xs_tok,
                                                     in1=dsk_bc[:, :, None].to_broadcast([128, 8, 64]), op=ALU.mult),
                     reads=[B_xsB, B_par], writes=[B_y2])
                c.at(T0 + 0.9)
                for h in range(8):
                    c.op(pe, lambda e: e.matmul(psW[0][:, h * 64:(h + 1) * 64], lhsT=GT[:, h, :],
                                                rhs=xdt[:, h * 64:(h + 1) * 64], start=True, stop=True),
                         reads=[B_GT, B_xdt], writes=[B_psW[0]], signal=(h == 7))
                c.op(pe, lambda e: e.matmul(psW[1][:], lhsT=CT, rhs=state_bf[:], start=True, stop=True),
                     reads=[B_xbcT, B_statebf], writes=[B_psW[1]])
                c.op(pe, lambda e: e.matmul(psW[2][:], lhsT=B_tok, rhs=xdtd[:], start=True, stop=True),
                     reads=[B_xsB, B_xdtd], writes=[B_psW[2]])
                c.at(T0 + 1.1)
                c.op(dve, lambda e: e.tensor_tensor(out=y1[:].rearrange("p (h d) -> p h d", h=8),
                                                    in0=psW[1][:].rearrange("p (h d) -> p h d", h=8),
                                                    in1=ecs[:, :, None].to_broadcast([128, 8, 64]), op=ALU.mult),
                     reads=[B_psW[1], B_dtt], writes=[B_y1])
                c.op(dve, lambda e: e.tensor_tensor(out=y1[:], in0=psW[0][:], in1=y1[:], op=ALU.add),
                     reads=[B_psW[0], B_y1], writes=[B_y1])
                c.op(dve, lambda e: e.tensor_tensor(out=state[:].rearrange("p (h d) -> p h d", h=8),
                                                    in0=state[:].rearrange("p (h d) -> p h d", h=8),
                                                    in1=cdec[:, :, None].to_broadcast([128, 8, 64]), op=ALU.mult),
                     reads=[B_state, B_dtt], writes=[B_state])
                c.op(dve, lambda e: e.tensor_tensor(out=state[:], in0=psW[2][:], in1=state[:], op=ALU.add),
                     reads=[B_psW[2], B_state], writes=[B_state])
                c.op(dve, lambda e: e.tensor_copy(out=state_bf[:], in_=state[:]), reads=[B_state], writes=[B_statebf])
                c.op(dve, lambda e: e.tensor_tensor(out=y1[:], in0=y1[:], in1=y2[:], op=ALU.add),
                     reads=[B_y1, B_y2], writes=[B_y1])
                c.op(dve, lambda e: e.tensor_tensor(out=y1[:], in0=y1[:], in1=zs[:], op=ALU.mult),
                     reads=[B_y1, B_zs], writes=[B_y1])
                c.op(dve, lambda e: e.memset(stB[:, 4:5], 0.0), writes=[B_stB])
                c.op(act, lambda e: e.activation(out=junkB[:], in_=y1[:], func=AF.Square, accum_out=stB[:, 4:5]),
                     reads=[B_y1], writes=[B_stB, B_junkB])
                c.op(act, lambda e: e.activation(out=stB[:, 5:6], in_=stB[:, 4:5], func=AF.Ln, scale=1.0 / 512, bias=EPS),
                     reads=[B_stB], writes=[B_stB])
                c.op(act, lambda e: e.activation(out=stB[:, 6:7], in_=stB[:, 5:6], func=AF.Exp, scale=-0.5),
                     reads=[B_stB], writes=[B_stB])
                c.op(dve, lambda e: e.scalar_tensor_tensor(out=yn_bf[:], in0=y1[:], scalar=stB[:, 6:7], in1=gssd[:],
                                                           op0=ALU.mult, op1=ALU.mult),
                     reads=[B_y1, B_stB, B_par], writes=[B_yn])
                c.at(T0 + 2.5)
                for cc in range(4):
                    c.op(pe, lambda e: e.transpose(psT[:, cc * 128:(cc + 1) * 128], yn_bf[:, cc * 128:(cc + 1) * 128], ident_bf[:]),
                         reads=[B_yn, B_const], writes=[B_psT], signal=(cc == 3))
                c.op(act, lambda e: e.copy(out=yT_st[:], in_=psT[:, 0:512].rearrange("p (k i) -> p k i", k=4)),
                     reads=[B_psT], writes=[B_yT])
                c.dma("sp", catT_s[t, :, 4:8, :], yT_st[:], reads=[B_yT], writes=[B_catY_tiles[t]])

            for t in range(NT):
                front(t)
                back(t)
                if t < 8:
                    c.at(t * PA + 2.2)
                    c.dma("pool", w_up_s[t * 128:(t + 1) * 128, :], w_up_d[t * 128:(t + 1) * 128, :], writes=[B_wus[t]])
                elif t < 16:
                    k4 = t - 8
                    c.at(t * PA + 2.2)
                    c.dma("pool", w_down_s[k4 * 512:(k4 + 1) * 512, :], w_down_d[k4 * 512:(k4 + 1) * 512, :],
                          writes=[B_wds[k4]])
            if "noreca" not in KSKIP:
                c.flush()
            c.barrier()

        if "B" in phases:
          with ExitStack() as ph:
            qTh = [sb(f"qz{i}", [128, 2, S], BF16, ph) for i in range(2)]
            kTh = [sb(f"kTh{i}", [128, S], BF16, ph) for i in range(2)]
            vTh = [sb(f"vTh{i}", [128, S], BF16, ph) for i in range(2)]
            B_qh = [[Buf(f"qkv{i}_{pp}") for pp in range(2)] for i in range(3)]
            acc_all = sb("acc_all", [128, 2, S], F32, ph)
            v_aug2 = [sb(f"v_aug{i}", [128, 32, 196], BF16, ph) for i in range(2)]
            B_vd2 = [Buf("vd0"), Buf("vd1")]
            attnT = sb("attnT", [128, S], BF16, ph)
            rd = sb("rd", [128, 512], F32, ph)
            NU = 3
            PTr = [sb(f"PTr{i}", [128, 2, 256], BF16, ph) for i in range(NU)]
            PT = [sb(f"PT{i}", [128, 2, 256], BF16, ph) for i in range(NU)]
            B_acc, B_attnT, B_rd = Buf("acc"), Buf("attnT"), Buf("rd")
            B_PTr = [Buf(f"PTr{i}") for i in range(NU)]
            B_PT = [Buf(f"PT{i}") for i in range(NU)]
            psTa = psum("psTa", [128, 1024], BF16, ph)
            psSs = [psum(f"psS{i}", [128, 2, 256], F32, ph) for i in range(NU)]
            psO = [psum(f"psO{i}", [128, 512], F32, ph) for i in range(2)]
            psR = psum("psR", [128, 512], F32, ph)
            psR2 = psum("psR2", [128, 512], F32, ph)
            B_psR2 = Buf("psR2")
            B_psTa = Buf("psTa")
            B_psTx = [B_psTa, B_psTa]
            B_psSs = [Buf(f"psS{i}") for i in range(NU)]
            B_psO = [Buf("psO0"), Buf("psO1")]
            B_psR = Buf("psR")
            psTx = [psTa, psTa]
            for i_ in range(2):
                c.op(dve, lambda e: e.memset(psO[i_][:], 0.0), writes=[B_psO[i_]])
            for v_aug, B_vd in zip(v_aug2, B_vd2):
                c.op(pool, lambda e: e.memset(v_aug[:], 0.0), writes=[B_vd])
                c.op(pool, lambda e: e.memset(v_aug[:, :, 64:65], 1.0), reads=[B_vd], writes=[B_vd])
                c.op(pool, lambda e: e.memset(v_aug[:, :, 100:101], 1.0), reads=[B_vd], writes=[B_vd])
            for pp in range(2):
                c.op(pool, lambda e: e.memset(qTh[pp][64:128, 0, :], 0.0), writes=[B_qh[0][pp]])
                c.op(pool, lambda e: e.memset(qTh[pp][0:64, 1, :], 0.0), writes=[B_qh[0][pp]])
            if "norec" not in KSKIP:
                c.begin_rec()

            def load_hp(hp, vt):
                c.at(vt)
                pp = hp % 2
                c.dma("sp", qTh[pp][0:64, 0, :], qkvT_s[0, hp, 0:64, :], reads=B_qkv_tiles, writes=[B_qh[0][pp]])
                c.dma("sp", qTh[pp][64:128, 1, :], qkvT_s[0, hp, 64:128, :], reads=B_qkv_tiles, writes=[B_qh[0][pp]])
                c.dma("sp", kTh[pp][:], qkvT_s[1, hp, :, :], reads=B_qkv_tiles, writes=[B_qh[1][pp]])
                c.dma("sp", vTh[pp][:], qkvT_s[2, hp, :, :], reads=B_qkv_tiles, writes=[B_qh[2][pp]])

            load_hp(0, -10.0)
            u = 0
            for hp in range(4):
                if hp + 1 < 4:
                    load_hp(hp + 1, u + 4.0)
                qT_, kT_, vT_ = qTh[hp % 2], kTh[hp % 2], vTh[hp % 2]
                Bq_, Bk_, Bv_ = B_qh[0][hp % 2], B_qh[1][hp % 2], B_qh[2][hp % 2]
                for di, d in enumerate((1, 4, 16)):
                    nb = 32 // d
                    g_ = hp * 3 + di
                    v_aug, B_vd = v_aug2[g_ % 2], B_vd2[g_ % 2]

                    def cols(r, a, n, d=d):
                        st_ = r + d * a
                        return slice(st_, st_ + d * (n - 1) + 1, d)

                    c.at(32.0 * (g_ - 1) + 2.2 if g_ > 0 else -1.0)
                    for blk in range(0 if "novaug" in KSKIP else 32):
                        r, j = blk // nb, blk % nb
                        grp = (blk // 4) % 2
                        c.op(pe, lambda e: e.transpose(psTx[grp][:, (blk % 4) * 128:(blk % 4 + 1) * 128],
                                                       vT_[:, cols(r, 128 * j, 128)], ident_bf[:]),
                             reads=[Bv_, B_const], writes=[B_psTx[grp]], signal=(blk % 4 == 3))
                        if blk % 4 == 3:
                            src = psTx[grp][:, 0:512].rearrange("p (k i) -> p k i", k=4)
                            c.op(act, lambda e: e.copy(out=v_aug[:, blk - 3:blk + 1, 0:64], in_=src[:, :, 0:64]),
                                 reads=[B_psTx[grp]], writes=[B_vd])
                            c.op(act, lambda e: e.copy(out=v_aug[:, blk - 3:blk + 1, 132:196], in_=src[:, :, 64:128]),
                                 reads=[B_psTx[grp]], writes=[B_vd])
                    for r in range(d):
                        for j in range(nb):
                            blk = r * nb + j
                            nq = 256 if j + 1 < nb else 128
                            Sp, BSp = psSs[u % NU], B_psSs[u % NU]
                            Pr, BPr = PTr[u % NU], B_PTr[u % NU]
                            Pt, BPt = PT[u % NU], B_PT[u % NU]
                            kc = cols(r, 128 * j, 128)
                            qc = cols(r, 128 * j, nq)
                            c.at(float(u))
                            if nq == 256:
                                c.op(pe, lambda e: e.matmul(Sp[:, :, 0:nq], lhsT=kT_[:, kc], rhs=qT_[:, :, qc],
                                                            start=True, stop=True),
                                     reads=[Bq_, Bk_], writes=[BSp], signal=True)
                            else:
                                for hh in range(2):
                                    c.op(pe, lambda e: e.matmul(Sp[:, hh, 0:nq], lhsT=kT_[:, kc], rhs=qT_[:, hh, qc],
                                                                start=True, stop=True),
                                         reads=[Bq_, Bk_], writes=[BSp], signal=(hh == 1))
                            if "noexp" not in KSKIP:
                              c.op(act, lambda e: e.activation(out=Pr[:, :, 0:nq], in_=Sp[:, :, 0:nq], func=AF.Exp, scale=0.125),
                                 reads=[BSp], writes=[BPr])
                            if "nomask" not in KSKIP:
                              c.op(dve, lambda e: e.tensor_tensor(out=Pt[:, :, 0:nq], in0=Pr[:, :, 0:nq],
                                                                 in1=maskAT_bf[:, None, 0:nq].to_broadcast([128, 2, nq]),
                                                                 op=ALU.mult),
                                 reads=[BPr, B_const], writes=[BPt])
                            c.at(u + NU - 0.5)
                            Oc, BOc = psO[j % 2], B_psO[j % 2]
                            if "pv" in KSKIP:
                                u += 1
                                continue
                            c.op(pe, lambda e: e.matmul(Oc[:, 128:256], lhsT=v_aug[:, blk, 68:196], rhs=Pt[:, 1, 0:128],
                                                        start=(j == 0), stop=True, skip_group_check=True),
                                 reads=[B_vd, BPt], writes=[BOc], signal=False)
                            c.op(pe, lambda e: e.matmul(Oc[0:65, 0:128], lhsT=v_aug[:, blk, 0:65], rhs=Pt[:, 0, 0:128],
                                                        start=False, stop=True, skip_group_check=True),
                                 reads=[B_vd, BPt], writes=[BOc], signal=True)
                            if j + 1 < nb:
                                On, BOn = psO[(j + 1) % 2], B_psO[(j + 1) % 2]
                                c.op(pe, lambda e: e.matmul(On[:, 128:256], lhsT=v_aug[:, blk, 68:196], rhs=Pt[:, 1, 128:256],
                                                            start=True, stop=False, skip_group_check=True),
                                     reads=[B_vd, BPt], writes=[BOn], signal=False)
                                c.op(pe, lambda e: e.matmul(On[0:65, 0:128], lhsT=v_aug[:, blk, 0:65], rhs=Pt[:, 0, 128:256],
                                                            start=False, stop=False, skip_group_check=True),
                                     reads=[B_vd, BPt], writes=[BOn], signal=True)
                            oc = cols(r, 128 * j, 128)
                            ov = Oc[:, 0:256].rearrange("p (a i) -> p a i", a=2)
                            if d == 1:
                                c.op(dve, lambda e: e.tensor_copy(out=acc_all[:, :, oc], in_=ov), reads=[BOc], writes=[B_acc])
                            else:
                                c.op(dve, lambda e: e.tensor_tensor(out=acc_all[:, :, oc], in0=ov, in1=acc_all[:, :, oc],
                                                                    op=ALU.add),
                                     reads=[BOc, B_acc], writes=[B_acc])
                            u += 1
                c.at(u + NU - 1.4)
                for cc in range(0 if "norm" in KSKIP else 8):
                    cs_ = slice(cc * 512, (cc + 1) * 512)
                    c.op(pe, lambda e: e.matmul(psR[0:64, :], lhsT=ones_f[64:65, 0:64], rhs=acc_all[64:65, 0, cs_],
                                                start=True, stop=True, skip_group_check=True),
                         reads=[B_acc, B_const], writes=[B_psR], signal=True)
                    c.op(pe, lambda e: e.matmul(psR2[64:128, :], lhsT=ones_f[32:33, 0:64], rhs=acc_all[32:33, 1, cs_],
                                                start=True, stop=True, skip_group_check=True),
                         reads=[B_acc, B_const], writes=[B_psR2], signal=True)
                    c.op(act, lambda e: e.activation(out=rd[0:64, :], in_=psR[0:64, :], func=AF.Ln), reads=[B_psR], writes=[B_rd])
                    c.op(act, lambda e: e.activation(out=rd[64:128, :], in_=psR2[64:128, :], func=AF.Ln),
                         reads=[B_psR2], writes=[B_rd])
                    c.op(act, lambda e: e.activation(out=rd[:], in_=rd[:], func=AF.Exp, scale=-1.0), reads=[B_rd], writes=[B_rd])
                    c.op(dve, lambda e: e.tensor_tensor(out=attnT[0:64, cs_], in0=acc_all[0:64, 0, cs_], in1=rd[0:64, :],
                                                        op=ALU.mult), reads=[B_rd, B_acc], writes=[B_attnT])
                    c.op(dve, lambda e: e.tensor_tensor(out=attnT[64:128, cs_], in0=acc_all[64:128, 1, cs_], in1=rd[64:128, :],
                                                        op=ALU.mult), reads=[B_rd, B_acc], writes=[B_attnT])
                c.dma("sp", catT_s[:, :, hp, :].rearrange("t p i -> p t i"), attnT[:].rearrange("p (t i) -> p t i", i=128),
                      reads=[B_attnT], writes=B_catA_tiles)
            if "norec" not in KSKIP:
                c.flush()
            c.barrier()

        esAB.close()
        if "C" in phases:
          with ExitStack() as ph:
            w_up_bf = sb("w_up_bf", [128, 8, 4096], BF16, ph)
            w_down_bf = sb("w_down_bf", [128, 32, D], BF16, ph)
            B_wup, B_wdn = Buf("wup"), Buf("wdn")
            rl = sb("rl", [128, D], F32, ph)
            tmp = sb("tmp", [128, D], F32, ph)
            B_rl, B_tmp = Buf("rl"), Buf("tmp")
            gpk = sb("gpk", [128, 8], F32, ph)
            B_gpk = Buf("gpk")
            c.dma("sp", gpk[:], g_mlp_pre_pk_d[:, :], writes=[B_gpk])
            catT = sb("catT", [128, 8, 128], BF16, ph)
            xh = [sb(f"xh{i}", [128, D], F32, ph) for i in range(2)]
            B_xh = [Buf("xh0"), Buf("xh1")]
            wvu = w_up_s.rearrange("(k p) n -> p k n", p=128)
            wvd = w_down_s.rearrange("(k p) n -> p k n", p=128)
            B_wupg = [Buf(f"wupg{g}") for g in range(4)]
            B_wdng = [Buf(f"wdng{g}") for g in range(4)]
            engs3 = [dve, act]
            ei = 0
            for g_ in range(4):
                cs_ = slice(g_ * 1024, (g_ + 1) * 1024)
                c.dma("sp", w_up_bf[:, :, cs_], wvu[:, :, cs_], reads=B_wus, writes=[B_wupg[g_]])
                for k4 in (2 * g_, 2 * g_ + 1):
                    c.dma("sp", w_down_bf[:, 4 * k4:4 * k4 + 4, :], wvd[:, 4 * k4:4 * k4 + 4, :], reads=[B_wds[k4]],
                          writes=[B_wdng[g_]])
                for k in range(8):
                    E_ = engs3[ei % 2]
                    ei += 1
                    if E_ is act:
                        c.op(act, lambda e: e.activation(out=w_up_bf[:, k, cs_], in_=w_up_bf[:, k, cs_], func=AF.Copy,
                                                         scale=gpk[:, k:k + 1]),
                             reads=[B_gpk, B_wupg[g_]], writes=[B_wupg[g_]])
                    else:
                        c.op(E_, lambda e: e.tensor_scalar(out=w_up_bf[:, k, cs_], in0=w_up_bf[:, k, cs_],
                                                           scalar1=gpk[:, k:k + 1], scalar2=None, op0=ALU.mult),
                             reads=[B_gpk, B_wupg[g_]], writes=[B_wupg[g_]])
            p_bf = sb("p_bf", [128, 256], BF16, ph)
            pT = sb("pT", [128, 2, 128], BF16, ph)
            u2T = sb("u2T", [128, 8, 128], BF16, ph)
            hT = sb("hT", [128, 8, 128], BF16, ph)
            a_bf = [sb(f"a_bf{i}", [128, D], BF16, ph) for i in range(2)]
            aT = sb("aT", [128, 8, 128], BF16, ph)
            junkC = sb("junkC", [128, 512], BF16, ph)
            stC = sb("stC", [128, 24], F32, ph)
            B_catT, B_pbf, B_pT = Buf("catT"), Buf("pbf"), Buf("pT")
            B_u2T, B_hT, B_aT, B_junkC = Buf("u2T"), Buf("hT"), Buf("aT"), Buf("junkC")
            B_a = [Buf("a0"), Buf("a1")]
            B_st = [Buf("stF"), Buf("stM"), Buf("stP")]
            psU = [[psum(f"psU{s_}{i}", [128, 512], F32, ph) for i in range(2)] for s_ in range(2)]
            B_psU = [[Buf(f"psU{s_}{i}") for i in range(2)] for s_ in range(2)]
            psF = [psum(f"psF{i}", [128, 512], F32, ph) for i in range(2)]
            B_psF = [Buf("psF0"), Buf("psF1")]
            psT2 = psum("psT2", [128, 1024], BF16, ph)
            B_psT2 = Buf("psT2")
            psS = psum("psS_", [128, 512], F32, ph)
            B_psS = Buf("psS")

            def rstd_chain(srcs, si):
                o = 8 * si
                Bs = B_st[si]
                c.op(dve, lambda e: e.memset(stC[:, o:o + 2], 0.0), writes=[Bs])
                for n, (ap_, Bap) in enumerate(srcs):
                    c.op(act, lambda e: e.activation(out=junkC[:], in_=ap_, func=AF.Square, accum_out=stC[:, o + n:o + n + 1]),
                         reads=[Bap], writes=[Bs, B_junkC])
                c.op(dve, lambda e: e.tensor_tensor(out=stC[:, o + 2:o + 3], in0=stC[:, o:o + 1], in1=stC[:, o + 1:o + 2],
                                                    op=ALU.add), reads=[Bs], writes=[Bs])
                c.op(act, lambda e: e.activation(out=stC[:, o + 2:o + 3], in_=stC[:, o + 2:o + 3], func=AF.Ln,
                                                 scale=1.0 / D, bias=EPS), reads=[Bs], writes=[Bs])
                c.op(act, lambda e: e.activation(out=stC[:, o + 3:o + 4], in_=stC[:, o + 2:o + 3], func=AF.Exp, scale=-0.5),
                     reads=[Bs], writes=[Bs])
                return stC[:, o + 3:o + 4], Bs

            def transposesN(src, Bsrc, dstT, BdstT, n=8):
                for k in range(n):
                    c.op(pe, lambda e: e.transpose(psT2[:, k * 128:(k + 1) * 128], src[:, k * 128:(k + 1) * 128], ident_bf[:]),
                         reads=[Bsrc, B_const], writes=[B_psT2], signal=(k == n - 1))
                c.op(act, lambda e: e.copy(out=dstT[:, 0:n, :], in_=psT2[:, 0:n * 128].rearrange("p (k i) -> p k i", k=n)),
                     reads=[B_psT2], writes=[BdstT])

            def proj2(ps2, Bps2, lhsT_t, BlhsT, w_t, Bw, nk, col0=0):
                for n in range(2):
                    for k in range(nk):
                        c.op(pe, lambda e: e.matmul(ps2[n][:], lhsT=lhsT_t[:, k, :],
                                                    rhs=w_t[:, k, col0 + n * 512:col0 + (n + 1) * 512],
                                                    start=(k == 0), stop=(k == nk - 1)),
                             reads=[BlhsT] + (list(Bw) if isinstance(Bw, (list, tuple)) else [Bw]), writes=[Bps2[n]],
                             signal=(k == nk - 1))

            PC = 10.0
            USET = [1, 0, 1, 1]
            c.begin_rec()
            for t in range(NT):
                rows = slice(t * 128, (t + 1) * 128)
                T0 = t * PC
                h, B_h = xh[t % 2], B_xh[t % 2]
                c.at(T0 - 6.5)
                c.dma("sp", catT[:], catT_s[t, :, :, :], reads=[B_catY_tiles[t], B_catA_tiles[t]], writes=[B_catT])
                c.dma("sp", h[:], x_d[rows, :], writes=[B_h])
                c.at(T0 - 3.5)
                proj2(psU[0], B_psU[0], catT, B_catT, w_out_bf, B_wg, 8)
                c.at(T0 - 2.0)
                r_, Br_ = rstd_chain([(psU[0][0][:], B_psU[0][0]), (psU[0][1][:], B_psU[0][1])], 0)
                for n in range(2):
                    hs = slice(n * 512, (n + 1) * 512)
                    c.op(dve, lambda e: e.scalar_tensor_tensor(out=tmp[:, hs], in0=psU[0][n][:], scalar=r_, in1=g_mix_post[:, hs],
                                                               op0=ALU.mult, op1=ALU.mult),
                         reads=[B_psU[0][n], Br_, B_gains], writes=[B_tmp])
                c.op(dve, lambda e: e.tensor_tensor(out=h[:], in0=h[:], in1=tmp[:], op=ALU.add),
                     reads=[B_h, B_tmp], writes=[B_h])
                r_, Br_ = rstd_chain([(h[:, 0:512], B_h), (h[:, 512:1024], B_h)], 0)
                c.op(dve, lambda e: e.tensor_scalar(out=a_bf[0][:], in0=h[:], scalar1=r_, scalar2=None, op0=ALU.mult),
                     reads=[B_h, Br_], writes=[B_a[0]])
                c.at(T0 - 0.5)
                transposesN(a_bf[0], B_a[0], u2T, B_u2T)
                for fg in range(4):
                    c.at(T0 + 2.0 * fg)
                    X, BX = psU[USET[fg]], B_psU[USET[fg]]
                    ab, Bab = a_bf[fg % 2], B_a[fg % 2]
                    proj2(X, BX, u2T, B_u2T, w_up_bf, B_wupg[fg], 8, col0=fg * 1024)
                    for n in range(2):
                        hs = slice(n * 512, (n + 1) * 512)
                        c.op(act, lambda e: e.activation(out=rl[:, hs], in_=X[n][:], func=AF.Relu), reads=[BX[n]], writes=[B_rl])
                    c.op(pool, lambda e: e.tensor_tensor(out=ab[:], in0=rl[:], in1=rl[:], op=ALU.mult),
                         reads=[B_rl], writes=[Bab])
                    c.at(T0 + 1.9 + 2.0 * fg)
                    transposesN(ab, Bab, aT, B_aT)
                    c.at(T0 + 3.0 + 2.0 * fg)
                    for n in range(2):
                        for i in range(8):
                            c.op(pe, lambda e: e.matmul(psF[n][:], lhsT=aT[:, i, :],
                                                        rhs=w_down_bf[:, fg * 8 + i, n * 512:(n + 1) * 512],
                                                        start=(fg == 0 and i == 0), stop=(fg == 3 and i == 7)),
                                 reads=[B_aT, B_wdng[fg]], writes=[B_psF[n]], signal=(i == 7))
                    if fg == 2:
                        c.at(T0 + 4.5)
                        c.dma("pool", p_bf[:], p_d[rows, :], writes=[B_pbf])
                c.at(T0 + 10.0)
                r_, Br_ = rstd_chain([(psF[0][:], B_psF[0]), (psF[1][:], B_psF[1])], 1)
                for n in range(2):
                    hs = slice(n * 512, (n + 1) * 512)
                    c.op(dve, lambda e: e.scalar_tensor_tensor(out=tmp[:, hs], in0=psF[n][:], scalar=r_, in1=g_mlp_post[:, hs],
                                                               op0=ALU.mult, op1=ALU.mult),
                         reads=[B_psF[n], Br_, B_gains], writes=[B_tmp])
                c.op(dve, lambda e: e.tensor_tensor(out=h[:], in0=h[:], in1=tmp[:], op=ALU.add),
                     reads=[B_h, B_tmp], writes=[B_h])
                c.op(dve, lambda e: e.tensor_copy(out=a_bf[1][:], in_=h[:]), reads=[B_h], writes=[B_a[1]])
                c.at(T0 + 10.5)
                transposesN(a_bf[1], B_a[1], hT, B_hT)
                transposesN(p_bf, B_pbf, pT, B_pT, n=2)
                gate_ps = [(psS, B_psS), (psU[1][0], B_psU[1][0])]
                proj_ps = [(psU[1][1], B_psU[1][1]), (psS, B_psS)]
                c.at(T0 + 11.0)
                for n in range(2):
                    hs = slice(n * 512, (n + 1) * 512)
                    gp, Bgp = gate_ps[n]
                    for k in range(8):
                        c.op(pe, lambda e: e.matmul(gp[:], lhsT=hT[:, k, :], rhs=w_gate_bf[:, k, hs], start=(k == 0), stop=(k == 7)),
                             reads=[B_hT, B_wg], writes=[Bgp], signal=(k == 7))
                    c.op(act, lambda e: e.activation(out=tmp[:, hs], in_=gp[:], func=AF.Sigmoid), reads=[Bgp], writes=[B_tmp])
                c.at(T0 + 11.5)
                for n in range(2):
                    hs = slice(n * 512, (n + 1) * 512)
                    pp_, Bpp = proj_ps[n]
                    for k in range(2):
                        c.op(pe, lambda e: e.matmul(pp_[:], lhsT=pT[:, k, :], rhs=w_proj_bf[:, k, hs], start=(k == 0), stop=(k == 1)),
                             reads=[B_pT, B_wg], writes=[Bpp], signal=(k == 1))
                    c.op(dve, lambda e: e.tensor_tensor(out=tmp[:, hs], in0=pp_[:], in1=tmp[:, hs], op=ALU.mult),
                         reads=[Bpp, B_tmp], writes=[B_tmp])
                c.at(T0 + 12.0)
                r_, Br_ = rstd_chain([(tmp[:, 0:512], B_tmp), (tmp[:, 512:1024], B_tmp)], 2)
                for n in range(2):
                    hs = slice(n * 512, (n + 1) * 512)
                    c.op(dve, lambda e: e.scalar_tensor_tensor(out=tmp[:, hs], in0=tmp[:, hs], scalar=r_, in1=g_ple_post[:, hs],
                                                               op0=ALU.mult, op1=ALU.mult),
                         reads=[B_tmp, Br_, B_gains], writes=[B_tmp])
                c.op(dve, lambda e: e.tensor_tensor(out=h[:], in0=h[:], in1=tmp[:], op=ALU.add),
                     reads=[B_h, B_tmp], writes=[B_h])
                c.at(T0 + 13.0)
                c.dma("sp", out_d[rows, :], h[:], reads=[B_h], writes=[B_out])
            c.flush()
            c.barrier()

        if debug:
            c.dma("sp", dbg["qkvT"][:, :, :, :], qkvT_s[:, :, :, :], reads=B_qkv_tiles)
            c.dma("sp", dbg["catT"][:, :, :, :], catT_s[:, :, :, :], reads=B_catY_tiles + B_catA_tiles)
        c.barrier()
        c.check_deadlock()
    return nc


def _prep_inputs(inputs):
    shared = {}
    f = lambda a: np.ascontiguousarray(np.asarray(a, dtype=np.float32))
    shared["norm_mix_pre"] = f(inputs["norm_mix_pre"][0:1])
    shared["norm_mix_post"] = f(inputs["norm_mix_post"][0:1])
    shared["w_in"] = f(inputs["w_in"][0])
    cw = np.asarray(inputs["conv_w"][0], dtype=np.float32)
    shared["conv_w"] = np.ascontiguousarray(cw.reshape(4, 6, 128).transpose(2, 1, 0))
    cb = np.asarray(inputs["conv_b"][0], dtype=np.float32)
    shared["conv_b"] = np.ascontiguousarray(cb.reshape(6, 128).T)
    shared["dt_bias"] = f(inputs["dt_bias"][0:1])
    shared["a_log"] = f(inputs["a_log"][0:1])
    shared["d_skip"] = f(inputs["d_skip"][0:1])
    shared["ssd_norm_g"] = f(inputs["ssd_norm_g"][0:1])
    shared["w_out"] = f(inputs["w_out"][0])
    shared["norm_mlp_pre_pk"] = np.ascontiguousarray(
        np.asarray(inputs["norm_mlp_pre"][0], dtype=np.float32).reshape(8, 128).T)
    shared["norm_mlp_post"] = f(inputs["norm_mlp_post"][0:1])
    shared["w_up"] = f(inputs["w_up"][0])
    shared["w_down"] = f(inputs["w_down"][0])
    shared["w_ple_gate"] = f(inputs["w_ple_gate"][0])
    shared["w_ple_proj"] = f(inputs["w_ple_proj"][0])
    shared["norm_ple_post"] = f(inputs["norm_ple_post"][0:1])
    x = np.asarray(inputs["x"], dtype=np.float32)
    p = np.asarray(inputs["p"], dtype=np.float32)
    pos = np.asarray(inputs["positions"], dtype=np.int32)
    maps = []
    for b in range(x.shape[0]):
        m = dict(shared)
        m["x"] = np.ascontiguousarray(x[b])
        m["p"] = np.ascontiguousarray(p[0, b])
        m["pos"] = np.ascontiguousarray(pos[b].reshape(NT, 128).T)
        maps.append(m)
    return maps


def kernel(**inputs):
    maps = _prep_inputs(inputs)
    nc = build_nc()
    res = run_bass_kernel_spmd(nc, maps, core_ids=list(range(len(maps))))
    return np.stack([r["out"] for r in res.results], axis=0)
```
